# Optimizing a Trainium2 kernel written in Bass

```python
import math
import jax, jax.numpy as jnp
from jax import lax
import numpy as np

D_MODEL = 1024
BATCH = 2
SEQ = 8192
DEPTH = 1

GRID_W = 64
CTX_LEN = 256
HEAD_DIM = 64
N_Q_HEADS = 8
N_KV_HEADS = 2
GROUP = N_Q_HEADS // N_KV_HEADS
WINDOW = 128
BLOCK = 128
ROPE_THETA = 10000.0
N_FOURIER_GROUPS = 4
FOURIER_GROUP_CH = 128
FOURIER_W = N_FOURIER_GROUPS * FOURIER_GROUP_CH
Q_W = N_Q_HEADS * HEAD_DIM
KV_W = N_KV_HEADS * HEAD_DIM
IN_W = Q_W + 2 * KV_W + FOURIER_W + 2 * D_MODEL
D_FF = 2816
CONV_WIDTH = 3
EPS = 1e-6
NEG = -1e30

kernel_name = "hybrid_swa_fnet_convffn_dit_block"


def rmsnorm(x, g):
    xf = x.astype(jnp.float32)
    xf = xf * lax.rsqrt(jnp.mean(xf * xf, axis=-1, keepdims=True) + EPS)
    return xf.astype(x.dtype) * g


def modulate(h, shift, scale):
    return h * (1 + scale) + shift


def adaln(cvec, w_mod, b_mod):
    return jnp.split(jax.nn.silu(cvec) @ w_mod + b_mod, 6, axis=-1)


def axial_rope_tables(rows, dtype):
    row = jnp.repeat(jnp.arange(rows), GRID_W).astype(jnp.float32)
    col = jnp.tile(jnp.arange(GRID_W), rows).astype(jnp.float32)
    half = HEAD_DIM // 2
    inv_freq = ROPE_THETA ** (-jnp.arange(0, half, 2, dtype=jnp.float32) / half)
    ang_r = (row[:, None] * inv_freq)[:, None, :]
    ang_c = (col[:, None] * inv_freq)[:, None, :]
    return tuple(t.astype(dtype) for t in (jnp.cos(ang_r), jnp.sin(ang_r), jnp.cos(ang_c), jnp.sin(ang_c)))


def rotate(x, cos, sin):
    x1, x2 = jnp.split(x, 2, axis=-1)
    return jnp.concatenate([x1 * cos - x2 * sin, x2 * cos + x1 * sin], axis=-1)


def apply_axial_rope(x, tabs):
    cos_r, sin_r, cos_c, sin_c = tabs
    xr, xc = jnp.split(x, 2, axis=-1)
    return jnp.concatenate([rotate(xr, cos_r, sin_r), rotate(xc, cos_c, sin_c)], axis=-1)


def split_proj(p):
    o1 = Q_W
    o2 = o1 + KV_W
    o3 = o2 + KV_W
    o4 = o3 + FOURIER_W
    o5 = o4 + D_MODEL
    return p[..., :o1], p[..., o1:o2], p[..., o2:o3], p[..., o3:o4], p[..., o4:o5], p[..., o5:]


def banded_window_attention(q, k, v, k_ctx, v_ctx, sink):
    B, S = q.shape[:2]
    L = k_ctx.shape[1]
    nb = S // BLOCK
    scale = HEAD_DIM ** -0.5
    qb = q.reshape(B, nb, BLOCK, N_KV_HEADS, GROUP, HEAD_DIM)

    def band(t):
        tp = jnp.pad(t, ((0, 0), (BLOCK, BLOCK), (0, 0), (0, 0)))
        tp = tp.reshape(B, nb + 2, BLOCK, N_KV_HEADS, HEAD_DIM)
        return jnp.concatenate([tp[:, :-2], tp[:, 1:-1], tp[:, 2:]], axis=2)

    kw, vw = band(k), band(v)
    s_loc = jnp.einsum('bnqhgd,bnkhd->bnhgqk', qb, kw).astype(jnp.float32) * scale
    s_ctx = jnp.einsum('bnqhgd,blhd->bnhgql', qb, k_ctx).astype(jnp.float32) * scale
    qi = jnp.arange(BLOCK)[:, None]
    kj = jnp.arange(3 * BLOCK)[None, :]
    rel = kj - BLOCK - qi
    kpos = jnp.arange(nb)[:, None, None] * BLOCK + kj[None] - BLOCK
    valid = (jnp.abs(rel) <= WINDOW)[None] & (kpos >= 0) & (kpos < S)
    s_loc = jnp.where(valid[None, :, None, None], s_loc, NEG)
    sink_l = jnp.broadcast_to(sink.astype(jnp.float32).reshape(1, 1, N_KV_HEADS, GROUP, 1, 1),
                              s_loc.shape[:-1] + (1,))
    p = jax.nn.softmax(jnp.concatenate([s_loc, s_ctx, sink_l], axis=-1), axis=-1)
    p_loc = p[..., :3 * BLOCK].astype(v.dtype)
    p_ctx = p[..., 3 * BLOCK:3 * BLOCK + L].astype(v.dtype)
    o = (jnp.einsum('bnhgqk,bnkhd->bnqhgd', p_loc, vw)
         + jnp.einsum('bnhgql,blhd->bnqhgd', p_ctx, v_ctx))
    return o.reshape(B, S, Q_W)


def context_attention(q, k, v, sink):
    B, L = q.shape[:2]
    scale = HEAD_DIM ** -0.5
    qg = q.reshape(B, L, N_KV_HEADS, GROUP, HEAD_DIM)
    s = jnp.einsum('blhgd,bmhd->bhglm', qg, k).astype(jnp.float32) * scale
    sink_l = jnp.broadcast_to(sink.astype(jnp.float32).reshape(1, N_KV_HEADS, GROUP, 1, 1),
                              s.shape[:-1] + (1,))
    p = jax.nn.softmax(jnp.concatenate([s, sink_l], axis=-1), axis=-1)[..., :L].astype(v.dtype)
    o = jnp.einsum('bhglm,bmhd->blhgd', p, v)
    return o.reshape(B, L, Q_W)


def fourier_mix(f):
    B, S = f.shape[:2]
    fg = f.reshape(B, S, N_FOURIER_GROUPS, FOURIER_GROUP_CH).astype(jnp.float32)
    out = jnp.fft.fft2(fg, axes=(1, 3), norm='ortho').real
    return out.reshape(B, S, FOURIER_W).astype(f.dtype)


def merge_branches(attn_o, four_o, g_a, g_f, w_pa, w_pf, w_out):
    m = jax.nn.sigmoid(g_a) * (attn_o @ w_pa) + jax.nn.sigmoid(g_f) * (four_o @ w_pf)
    return m @ w_out


def conv_ffn(h, w_up, conv_w, conv_b, w_down):
    u, gate = jnp.split(h @ w_up, 2, axis=-1)
    up = jnp.pad(u, ((0, 0), (1, 1), (0, 0)))
    u = up[:, :-2] * conv_w[0] + up[:, 1:-1] * conv_w[1] + up[:, 2:] * conv_w[2] + conv_b
    return (jax.nn.silu(u) * gate) @ w_down


def setup_inputs(seed: int = 0) -> dict:
    key = jax.random.key(seed)
    ks = jax.random.split(key, 19)
    f32 = jnp.float32
    D = D_MODEL

    def nrm(k, shape, s):
        return jax.random.normal(k, shape, f32) * s

    return {
        'x': nrm(ks[0], (BATCH, SEQ, D), 1.0),
        'c': nrm(ks[1], (BATCH, D), 1.0),
        'ctx': nrm(ks[2], (BATCH, CTX_LEN, D), 1.0),
        'c_ctx': nrm(ks[3], (D,), 1.0),
        'w_mod': nrm(ks[4], (DEPTH, D, 6 * D), 0.5 * D ** -0.5),
        'b_mod': nrm(ks[5], (DEPTH, 6 * D), 0.01),
        'g_pre1': 1.0 + nrm(ks[6], (DEPTH, D), 0.05),
        'g_post1': 1.0 + nrm(ks[7], (DEPTH, D), 0.05),
        'g_pre2': 1.0 + nrm(ks[8], (DEPTH, D), 0.05),
        'g_post2': 1.0 + nrm(ks[9], (DEPTH, D), 0.05),
        'w_in': nrm(ks[10], (DEPTH, D, IN_W), D ** -0.5),
        'sink': nrm(ks[11], (DEPTH, N_Q_HEADS), 0.5),
        'w_pa': nrm(ks[12], (DEPTH, Q_W, D), Q_W ** -0.5),
        'w_pf': nrm(ks[13], (DEPTH, FOURIER_W, D), FOURIER_W ** -0.5),
        'w_out': nrm(ks[14], (DEPTH, D, D), D ** -0.5),
        'w_up': nrm(ks[15], (DEPTH, D, 2 * D_FF), D ** -0.5),
        'conv_w': nrm(ks[16], (DEPTH, CONV_WIDTH, D_FF), CONV_WIDTH ** -0.5),
        'conv_b': nrm(ks[17], (DEPTH, D_FF), 0.01),
        'w_down': nrm(ks[18], (DEPTH, D_FF, D), D_FF ** -0.5),
    }


def reference(x, c, ctx, c_ctx, w_mod, b_mod, g_pre1, g_post1, g_pre2, g_post2,
              w_in, sink, w_pa, w_pf, w_out, w_up, conv_w, conv_b, w_down):
    B, S, _ = x.shape
    L = ctx.shape[1]
    ROWS = S // GRID_W
    tabs = axial_rope_tables(ROWS, x.dtype)
    for l in range(DEPTH):
        last = l == DEPTH - 1
        sh1, sc1, gt1, sh2, sc2, gt2 = [m[:, None] for m in adaln(c, w_mod[l], b_mod[l])]
        csh1, csc1, cgt1, csh2, csc2, cgt2 = adaln(c_ctx, w_mod[l], b_mod[l])

        h = modulate(rmsnorm(x, g_pre1[l]), sh1, sc1)
        hc = modulate(rmsnorm(ctx, g_pre1[l]), csh1, csc1)
        q, k, v, f, g_a, g_f = split_proj(h @ w_in[l])
        q = apply_axial_rope(q.reshape(B, S, N_Q_HEADS, HEAD_DIM), tabs)
        k = apply_axial_rope(k.reshape(B, S, N_KV_HEADS, HEAD_DIM), tabs)
        v = v.reshape(B, S, N_KV_HEADS, HEAD_DIM)
        if last:
            pc = hc @ w_in[l, :, :Q_W + 2 * KV_W]
        else:
            pc = hc @ w_in[l]
        kc = pc[..., Q_W:Q_W + KV_W].reshape(B, L, N_KV_HEADS, HEAD_DIM)
        vc = pc[..., Q_W + KV_W:Q_W + 2 * KV_W].reshape(B, L, N_KV_HEADS, HEAD_DIM)

        attn = banded_window_attention(q, k, v, kc, vc, sink[l])
        y = merge_branches(attn, fourier_mix(f), g_a, g_f, w_pa[l], w_pf[l], w_out[l])
        x = x + gt1 * rmsnorm(y, g_post1[l])

        if not last:
            qc, _, _, fc, g_ac, g_fc = split_proj(pc)
            attn_c = context_attention(qc.reshape(B, L, N_Q_HEADS, HEAD_DIM), kc, vc, sink[l])
            yc = merge_branches(attn_c, fourier_mix(fc), g_ac, g_fc, w_pa[l], w_pf[l], w_out[l])
            ctx = ctx + cgt1 * rmsnorm(yc, g_post1[l])

        h2 = modulate(rmsnorm(x, g_pre2[l]), sh2, sc2)
        x = x + gt2 * rmsnorm(conv_ffn(h2, w_up[l], conv_w[l], conv_b[l], w_down[l]), g_post2[l])

        if not last:
            hc2 = modulate(rmsnorm(ctx, g_pre2[l]), csh2, csc2)
            ctx = ctx + cgt2 * rmsnorm(conv_ffn(hc2, w_up[l], conv_w[l], conv_b[l], w_down[l]), g_post2[l])
    return x
```

```python
import math
import numpy as np
import ml_dtypes
import concourse.bass as bass
import concourse.mybir as mybir
from concourse.bass_utils import run_bass_kernel_spmd

F32 = mybir.dt.float32
BF16 = mybir.dt.bfloat16
AF = mybir.ActivationFunctionType
ALU = mybir.AluOpType

D = 1024
S = 8192
NBLK = 64
L = 256
DFF = 2816
NJ = DFF // 128
EPS = 1e-6
NWIN = 20
NQB = 18
WCOLS = 3456
QOFF, QPOFF, KOFF, KPOFF, VOFF, GAOFF, GFOFF = 0, 512, 1024, 1152, 1280, 1408, 2432
NVEC = 200
NEGM = -30000.0

SAME_ENG_SYNC = True


class Sched:
    COMPUTE = ("pe", "act", "dve", "pool")
    DMAQ = ("sp", "gq")
    RING = 8

    def __init__(self):
        self.ops = []
        self.last_w = {}
        self.readers = {}
        self.per_eng = {e: [] for e in self.COMPUTE + self.DMAQ}

    def op(self, eng, fn, reads=(), writes=(), extra=()):
        oid = len(self.ops)
        deps = set(extra)
        for k in reads:
            w = self.last_w.get(k)
            if w is not None:
                deps.add(w)
            if isinstance(k, tuple) and k[0] == "bk":
                for r_ in self.readers.get(k, ()):
                    if self.ops[r_]["eng"] != eng:
                        deps.add(r_)
        for k in writes:
            w = self.last_w.get(k)
            if w is not None:
                deps.add(w)
            for r_ in self.readers.get(k, ()):
                deps.add(r_)
        for k in reads:
            self.readers.setdefault(k, []).append(oid)
        for k in writes:
            self.last_w[k] = oid
            self.readers[k] = []
        deps.discard(oid)
        self.ops.append(dict(eng=eng, fn=fn, deps=deps, idx=len(self.per_eng[eng])))
        self.per_eng[eng].append(oid)
        return oid

    def barrier(self):
        tails = []
        for e, lst in self.per_eng.items():
            if e in self.DMAQ:
                tails += lst[-self.RING:]
            elif lst:
                tails.append(lst[-1])
        ids = []
        for e in self.COMPUTE + self.DMAQ:
            ids.append(self.op(e, None, extra=tails))
        return ids

    def emit(self, nc, block, sems):
        ops = self.ops
        for o in ops:
            o["sig"] = False
        for o in ops:
            for d in o["deps"]:
                od = ops[d]
                if od["eng"] == o["eng"] and o["eng"] == "pe":
                    continue
                if od["eng"] == o["eng"] and o["eng"] in self.COMPUTE and not SAME_ENG_SYNC:
                    continue
                od["sig"] = True
        for e in self.COMPUTE:
            cnt = 0
            for oid in self.per_eng[e]:
                if ops[oid]["sig"]:
                    cnt += 1
                    ops[oid]["val"] = cnt
            nxt = None
            for oid in reversed(self.per_eng[e]):
                if ops[oid]["sig"]:
                    nxt = ops[oid]["val"]
                ops[oid]["cval"] = nxt
        for e in self.DMAQ:
            real = [oid for oid in self.per_eng[e] if ops[oid]["fn"] is not None]
            for j, oid in enumerate(real):
                ops[oid]["dsem"] = j % self.RING
                ops[oid]["dval"] = 16 * (j // self.RING + 1)
                ops[oid]["dprev"] = real[j - self.RING] if j >= self.RING else None

        def waits_for(o):
            need = {}
            deps = set(o["deps"])
            if o["eng"] in self.DMAQ and o["fn"] is not None and o.get("dprev") is not None:
                deps.add(o["dprev"])
            for d in deps:
                od = ops[d]
                if od["fn"] is None and od["eng"] in self.DMAQ:
                    continue
                if od["eng"] in self.DMAQ:
                    key = (od["eng"], od["dsem"])
                    val = od["dval"]
                else:
                    if od["eng"] == o["eng"]:
                        if o["eng"] == "pe" or not SAME_ENG_SYNC:
                            continue
                    key = (od["eng"], 0)
                    val = od["cval"]
                    assert val is not None
                if need.get(key, 0) < val:
                    need[key] = val
            return need

        def run(ename, eng_handle):
            known = {}
            for oid in self.per_eng[ename]:
                o = ops[oid]
                need = waits_for(o)
                for key, val in sorted(need.items()):
                    if known.get(key, 0) >= val:
                        continue
                    known[key] = val
                    eng_handle.wait_ge(sems[key], val)
                if o["fn"] is None:
                    if ename in self.COMPUTE and o["sig"]:
                        eng_handle.nop().then_inc(sems[(ename, 0)], 1)
                    continue
                ins = o["fn"](eng_handle)
                if ename in self.DMAQ:
                    ins.then_inc(sems[(ename, o["dsem"])], 16)
                elif o["sig"]:
                    ins.then_inc(sems[(ename, 0)], 1)

        @block.tensor
        def _(e):
            run("pe", e)

        @block.scalar
        def _(e):
            run("act", e)

        @block.vector
        def _(e):
            run("dve", e)

        @block.gpsimd
        def _(e):
            merged = sorted(self.per_eng["pool"] + self.per_eng["gq"])
            known = {}
            for oid in merged:
                o = ops[oid]
                need = waits_for(o)
                for key, val in sorted(need.items()):
                    if known.get(key, 0) >= val:
                        continue
                    known[key] = val
                    e.wait_ge(sems[key], val)
                if o["fn"] is None:
                    if o["eng"] == "pool" and o["sig"]:
                        e.nop().then_inc(sems[("pool", 0)], 1)
                    continue
                ins = o["fn"](e)
                if o["eng"] == "gq":
                    ins.then_inc(sems[("gq", o["dsem"])], 16)
                elif o["sig"]:
                    ins.then_inc(sems[("pool", 0)], 1)

        @block.sync
        def _(e):
            run("sp", e)


class Arena:
    def __init__(self, big, nbytes):
        self.big = big
        self.nbytes = nbytes
        self.off = 0
        self.marks = []

    def alloc(self, shape, dtype):
        esz = 2 if dtype == BF16 else 4
        n = 1
        for s in shape[1:]:
            n *= s
        nb = (n * esz + 63) // 64 * 64
        assert self.off + nb <= getattr(self, "top_lo", self.nbytes), ("arena overflow", self.off, nb, self.nbytes)
        v = self.big[:, self.off // 4:(self.off + nb) // 4]
        self.off += nb
        if dtype == BF16:
            v = v.bitcast(BF16)
        v = v[:, 0:n]
        if len(shape) == 2:
            return v
        if len(shape) == 3:
            return v.rearrange("p (a b) -> p a b", b=shape[2])
        if len(shape) == 4:
            return v.rearrange("p (a b c) -> p a b c", b=shape[2], c=shape[3])
        raise ValueError(shape)

    def carve_top(self, shape, dtype):
        esz = 2 if dtype == BF16 else 4
        n = 1
        for s_ in shape[1:]:
            n *= s_
        nb = (n * esz + 63) // 64 * 64
        lo = self.nbytes - nb
        v = self.big[:, lo // 4:self.nbytes // 4]
        if dtype == BF16:
            v = v.bitcast(BF16)
        v = v[:, 0:n]
        self.top_lo = lo
        return v.rearrange("p (a b) -> p a b", b=shape[2]) if len(shape) == 3 else v

    def mark(self):
        self.marks.append(self.off)

    def release(self):
        self.off = self.marks.pop()


def build_program(debug=False):
    nc = bass.Bass("TRN2", target_bir_lowering=False)

    def din(name, shape, dt=F32):
        return nc.dram_tensor(name, list(shape), dt, kind="ExternalInput").ap()

    x_full = din("x_full", [S, D])
    x_win = din("x_win", [NWIN * 128, D])
    ctxb = din("ctxb", [L, D])
    vecs = din("vecs", [128, NVEC])
    w_mod = din("w_mod", [D, 6 * D])
    w_f = din("w_f", [D, 512])
    w_W = din("w_W", [D, WCOLS])
    w_pa = din("w_pa", [512, D])
    w_pf = din("w_pf", [512, D])
    w_out = din("w_out", [D, D])
    w_upr = din("w_upr", [NJ, 2, 128, 8, 128])
    w_down = din("w_down", [DFF, D])
    t_w1 = din("t_w1", [128, 128], BF16)
    t_w2 = din("t_w2", [128, 2, 128], BF16)
    t_w3 = din("t_w3", [128, 64, 2, 72], BF16)
    t_cos = din("t_cos", [128, NWIN * 128])
    t_sin = din("t_sin", [128, NWIN * 128])
    t_mask = din("t_mask", [128, 4, 512], BF16)
    t_idb = din("t_idb", [128, 128], BF16)
    t_idf = din("t_idf", [128, 128])
    out = nc.dram_tensor("out", [2048, D], F32, kind="ExternalOutput").ap()
    skind = dict(kind="ExternalOutput") if debug else {}
    F_dram = nc.dram_tensor("F_scr", [NBLK, 4, 128, 128], BF16, **skind).ap()
    SG_dram = nc.dram_tensor("SG_scr", [NQB * 128, 2048], BF16, **skind).ap()
    X1_dram = nc.dram_tensor("X1_scr", [2048, D], F32, **skind).ap()
    H2_dram = nc.dram_tensor("H2_scr", [128, 8, 2048], BF16, **skind).ap()
    if debug:
        d_dv = nc.dram_tensor("d_dv", [128, 128], F32, kind="ExternalOutput").ap()
        d_gg1 = nc.dram_tensor("d_gg1", [128, 1024], F32, kind="ExternalOutput").ap()
        d_R = nc.dram_tensor("d_R", [128, 4 * NQB * 128], BF16, kind="ExternalOutput").ap()
        d_kT = nc.dram_tensor("d_kT", [128, NWIN * 128 + L], BF16, kind="ExternalOutput").ap()
        d_qT = nc.dram_tensor("d_qT", [128, NQB * 512], BF16, kind="ExternalOutput").ap()
        d_vA = nc.dram_tensor("d_vA", [128, (NWIN + 2) * 130], BF16, kind="ExternalOutput").ap()
        d_hl = nc.dram_tensor("d_hl", [128, 16], BF16, kind="ExternalOutput").ap()
        d_x63 = nc.dram_tensor("d_x63", [128, 1024], F32, kind="ExternalOutput").ap()
        d_xn63 = nc.dram_tensor("d_xn63", [128, 1024], BF16, kind="ExternalOutput").ap()
        d_hT63 = nc.dram_tensor("d_hT63", [128, 1024], BF16, kind="ExternalOutput").ap()
        d_F63 = nc.dram_tensor("d_F63", [128, 512], BF16, kind="ExternalOutput").ap()

    S_ = Sched()
    ARENA_BYTES = 200 * 1024

    from contextlib import ExitStack
    with ExitStack() as es:
        big = es.enter_context(nc.sbuf_tensor("arena", [128, ARENA_BYTES // 4], F32))
        psall = es.enter_context(nc.psum_tensor("psall", [128, 4096], F32))
        sems = {}
        for e in Sched.COMPUTE:
            sems[(e, 0)] = es.enter_context(nc.semaphore("s_" + e))
        for e in Sched.DMAQ:
            for j in range(Sched.RING):
                sems[(e, j)] = es.enter_context(nc.semaphore("s_%s%d" % (e, j)))
        block = es.enter_context(nc.Block())

        A = Arena(big, ARENA_BYTES)

        def bank(i):
            return psall[:, i * 512:(i + 1) * 512]

        def bankb(i):
            return psall[:, i * 512:(i + 1) * 512].bitcast(BF16)

        vec = A.alloc([128, NVEC], F32)
        idb = A.alloc([128, 128], BF16)
        idf = A.alloc([128, 128], F32)
        onesf = A.alloc([128, 128], F32)
        modT = A.alloc([128, 48, 2], F32)
        dv = A.alloc([128, 16, 8], F32)
        gg1 = A.alloc([128, 1024], F32)
        gg2 = A.alloc([128, 1024], F32)
        Rt = A.alloc([128, 4, NQB * 128], BF16)
        junk = A.alloc([128, 1024], BF16)
        small = A.alloc([128, 64], F32)
        eps_t = A.alloc([128, 1], F32)
        h2halo = A.alloc([128, 8, 2], BF16)
        (SC1, SH1, CSC1, CSH1, SC2, SH2, SC2L, SH2L, SC2R, SH2R, GG1, GG2, ESINK) = range(13)

        def dve(fn, r, w):
            return S_.op("dve", fn, r, w)

        def act(fn, r, w):
            return S_.op("act", fn, r, w)

        def pool(fn, r, w):
            return S_.op("pool", fn, r, w)

        def pe(fn, r, w):
            return S_.op("pe", fn, r, w)

        def sp(fn, r, w):
            return S_.op("sp", fn, r, w)

        def gq(fn, r, w):
            return S_.op("gq", fn, r, w)

        sp(lambda e: e.dma_start(out=vec, in_=vecs), [], ["vec"])
        sp(lambda e: e.dma_start(out=idb, in_=t_idb), [], ["idb"])
        sp(lambda e: e.dma_start(out=idf, in_=t_idf), [], ["idf"])
        dve(lambda e: e.memset(onesf, 1.0), [], ["onesf"])
        dve(lambda e: e.memset(eps_t, EPS), [], ["eps"])

        scb = A.alloc([128, 8, 2], BF16)
        A.mark()
        wmb = [A.alloc([128, 2048], BF16) for _ in range(2)]
        act(lambda e: e.activation(out=scb.rearrange("p a b -> p (a b)"), in_=vec[:, 0:16], func=AF.Silu),
            ["vec"], ["scb"])
        dve(lambda e: e.tensor_copy(out=modT, in_=vec[:, 16:64].unsqueeze(2).broadcast_to([128, 48, 2])),
            ["vec"], ["modT"])
        wst = A.alloc([128, 2048], F32)
        for k in range(8):
            sl = k % 2
            if sl == 0:
                gq(lambda e, k=k, sl=sl: e.dma_start(out=wmb[sl], in_=w_mod[k * 128:(k + 1) * 128, 0:2048]),
                   [], [("wmb", sl)])
            else:
                sp(lambda e, k=k: e.dma_start(out=wst, in_=w_mod[k * 128:(k + 1) * 128, 0:2048]), [], ["wst"])
                dve(lambda e, sl=sl: e.tensor_copy(out=wmb[sl], in_=wst), ["wst"], [("wmb", sl)])
            for m in range(16):
                pe(lambda e, k=k, m=m, sl=sl: e.matmul(
                    bank(sl)[:, 2 * m:2 * m + 2], lhsT=wmb[sl][:, m * 128:(m + 1) * 128], rhs=scb[:, k, :],
                    start=True, stop=True),
                   [("wmb", sl), "scb"], [("bk", sl)])
            dve(lambda e, sl=sl: e.tensor_tensor(
                out=modT[:, 0:16, :], in0=bank(sl)[:, 0:32].rearrange("p (m t) -> p m t", t=2), in1=modT[:, 0:16, :],
                op=ALU.add), [("bk", sl), "modT"], ["modT"])
        G_PRE1, G_POST1, G_PRE2, G_POST2 = 64, 72, 80, 88

        def mod(sec, which):
            return modT[:, sec * 8:(sec + 1) * 8, which]

        def dvs(i):
            return dv[:, i, :]

        def mk_scale(dst, sec, which, goff, mk="modT"):
            dve(lambda e: e.tensor_tensor(out=dvs(dst), in0=mod(sec, which), in1=vec[:, goff:goff + 8], op=ALU.mult),
                [mk, "vec"], [("dv", dst)])
            dve(lambda e: e.tensor_tensor(out=dvs(dst), in0=dvs(dst), in1=vec[:, goff:goff + 8], op=ALU.add),
                [("dv", dst), "vec"], [("dv", dst)])

        mk_scale(SC1, 1, 0, G_PRE1)
        mk_scale(CSC1, 1, 1, G_PRE1)
        dve(lambda e: e.tensor_copy(out=dvs(SH1), in_=mod(0, 0)), ["modT"], [("dv", SH1)])
        dve(lambda e: e.tensor_copy(out=dvs(CSH1), in_=mod(0, 1)), ["modT"], [("dv", CSH1)])
        act(lambda e: e.activation(out=dvs(ESINK), in_=vec[:, 184:192], func=AF.Exp), ["vec"], [("dv", ESINK)])
        A.release()
        S_.barrier()

        def adaln_group_b(wmB, diag):
            for k in range(8):
                sl = 6 + (k % 2)
                for m in range(32):
                    pe(lambda e, k=k, m=m, sl=sl: e.matmul(
                        bank(sl)[:, 2 * m:2 * m + 2], lhsT=wmB[:, k, m * 128:(m + 1) * 128], rhs=scb[:, k, :],
                        start=True, stop=True), [("wmB", k), "scb"], [("bk", sl)])
                dve(lambda e, sl=sl: e.tensor_tensor(
                    out=modT[:, 16:48, :], in0=bank(sl)[:, 0:64].rearrange("p (m t) -> p m t", t=2),
                    in1=modT[:, 16:48, :], op=ALU.add), [("bk", sl), "modTB"], ["modTB"])
            mk_scale(SC2, 4, 0, G_PRE2, mk="modTB")
            dve(lambda e: e.tensor_copy(out=dvs(SH2), in_=mod(3, 0)), ["modTB"], [("dv", SH2)])
            for dst, src, fl in ((SC2L, SC2, 192), (SH2L, SH2, 192), (SC2R, SC2, 193), (SH2R, SH2, 193)):
                dve(lambda e, dst=dst, src=src, fl=fl: e.tensor_scalar(
                    out=dvs(dst), in0=dvs(src), scalar1=vec[:, fl:fl + 1], scalar2=None, op0=ALU.mult),
                    [("dv", src), "vec"], [("dv", dst)])
            dve(lambda e: e.tensor_tensor(out=dvs(GG1), in0=mod(2, 0), in1=vec[:, G_POST1:G_POST1 + 8], op=ALU.mult),
                ["modTB", "vec"], [("dv", GG1)])
            dve(lambda e: e.tensor_tensor(out=dvs(GG2), in0=mod(5, 0), in1=vec[:, G_POST2:G_POST2 + 8], op=ALU.mult),
                ["modTB", "vec"], [("dv", GG2)])
            n_ = 0
            for (src, dstt, nm) in ((GG1, gg1, "gg1"), (GG2, gg2, "gg2")):
                for c in range(8):
                    sl = n_ % 2
                    bk = 6 + (n_ % 2)
                    n_ += 1
                    dve(lambda e, src=src, c=c, sl=sl: e.tensor_scalar(
                        out=diag[:, sl, :], in0=idf, scalar1=dv[:, src, c:c + 1], scalar2=None, op0=ALU.mult),
                        [("dv", src), "idf"], [("diag", sl)])
                    pe(lambda e, sl=sl, bk=bk: e.matmul(bank(bk)[:, 0:128], lhsT=onesf, rhs=diag[:, sl, :],
                                                        start=True, stop=True),
                       [("diag", sl), "onesf"], [("bk", bk)])
                    act(lambda e, dstt=dstt, c=c, bk=bk: e.copy(out=dstt[:, c * 128:(c + 1) * 128],
                                                               in_=bank(bk)[:, 0:128]), [("bk", bk)], [nm])
            if debug:
                sp(lambda e: e.dma_start(out=d_dv, in_=dv.rearrange("p a b -> p (a b)")),
                   [("dv", i) for i in range(13)], ["d_dv"])
                sp(lambda e: e.dma_start(out=d_gg1, in_=gg1), ["gg1"], ["d_gg1"])

        def run_pipeline(nitems, stages):
            ns = len(stages)
            for t in range(nitems + ns - 1):
                for st_ in reversed(range(ns)):
                    i = t - st_
                    if 0 <= i < nitems:
                        stages[st_](i)

        class Ring:
            def __init__(self, name, bufs):
                self.name = name
                self.bufs = bufs

            def __call__(self, i):
                return self.bufs[i % len(self.bufs)]

            def key(self, i):
                return (self.name, i % len(self.bufs))

        def norm_stages(items, xt, xn, hT, tpbanks, smbase, act_share=1, plain=False, merge=False):
            def ssc(i):
                return smbase + 2 * (i % 4)

            def s0(i):
                sp(lambda e: e.dma_start(out=xt(i), in_=items[i][0]), [], [xt.key(i)])

            def s1(i):
                c = ssc(i)
                act(lambda e: e.activation(out=junk, in_=xt(i), func=AF.Square, accum_out=small[:, c:c + 1]),
                    [xt.key(i)], [("sm", c), "junk"])
                act(lambda e: e.activation(out=small[:, c + 1:c + 2], in_=small[:, c:c + 1], func=AF.Sqrt,
                                           scale=1.0 / D, bias=eps_t[:, 0:1]), [("sm", c), "eps"], [("sm", c + 1)])

            def s2(i):
                c = ssc(i)
                dve(lambda e: e.reciprocal(out=small[:, c + 1:c + 2], in_=small[:, c + 1:c + 2]),
                    [("sm", c + 1)], [("sm", c + 1)])

            def s3(i):
                c = ssc(i)
                act(lambda e: e.activation(out=xn(i), in_=xt(i), func=AF.Copy, scale=small[:, c + 1:c + 2]),
                    [xt.key(i), ("sm", c + 1)], [xn.key(i)])

            def s4(i):
                tb = tpbanks[i % len(tpbanks)]
                for k in range(8):
                    pe(lambda e, k=k: e.transpose(out=bankb(tb)[:, k * 128:(k + 1) * 128],
                                                  in_=xn(i)[:, k * 128:(k + 1) * 128], identity=idb),
                       [xn.key(i), "idb"], [("bk", tb)])

            def s5(i):
                tb = tpbanks[i % len(tpbanks)]
                sc_i, sh_i = items[i][1], items[i][2]
                if plain:
                    dve(lambda e: e.tensor_copy(out=hT(i).rearrange("p a b -> p (a b)"), in_=bankb(tb)),
                        [("bk", tb)], [hT.key(i)])
                    return
                for k in range(8):
                    src = bankb(tb)[:, k * 128:(k + 1) * 128]
                    dst = hT(i)[:, k, :]
                    if i % 4 != 3 or act_share == 0:
                        dve(lambda e, src=src, dst=dst, k=k: e.tensor_scalar(
                            out=dst, in0=src, scalar1=dv[:, sc_i, k:k + 1], scalar2=dv[:, sh_i, k:k + 1],
                            op0=ALU.mult, op1=ALU.add),
                            [("bk", tb), ("dv", sc_i), ("dv", sh_i)], [hT.key(i)])
                    else:
                        act(lambda e, src=src, dst=dst, k=k: e.activation(
                            out=dst, in_=src, func=AF.Identity, scale=dv[:, sc_i, k:k + 1], bias=dv[:, sh_i, k:k + 1]),
                            [("bk", tb), ("dv", sc_i), ("dv", sh_i)], [hT.key(i)])

            if merge:
                def s123(i):
                    s1(i)
                    s2(i)
                    s3(i)
                return [s0, s123, s4, s5]
            return [s0, s1, s2, s3, s4, s5]

        def norm_to_hT(tag, xt_ap, xt_key, xn, xn_key, hT_dst, hT_key, sc_i, sh_i, tpb, ssi, cols=None):
            ss = small[:, ssi:ssi + 1]
            rs = small[:, ssi + 1:ssi + 2]
            act(lambda e: e.activation(out=junk, in_=xt_ap, func=AF.Square, accum_out=ss),
                [xt_key], [("sm", ssi)])
            act(lambda e: e.activation(out=rs, in_=ss, func=AF.Sqrt, scale=1.0 / D, bias=eps_t[:, 0:1]),
                [("sm", ssi), "eps"], [("sm", ssi + 1)])
            dve(lambda e: e.reciprocal(out=rs, in_=rs), [("sm", ssi + 1)], [("sm", ssi + 1)])
            act(lambda e: e.activation(out=xn, in_=xt_ap, func=AF.Copy, scale=rs),
                [xt_key, ("sm", ssi + 1)], [xn_key])
            for k in range(8):
                pe(lambda e, k=k: e.transpose(out=bankb(tpb)[:, k * 128:(k + 1) * 128],
                                              in_=xn[:, k * 128:(k + 1) * 128], identity=idb),
                   [xn_key, "idb"], [("bk", tpb)])
            for k in range(8):
                if cols is None:
                    src = bankb(tpb)[:, k * 128:(k + 1) * 128]
                    dst = hT_dst[:, k, :]
                else:
                    src = bankb(tpb)[:, k * 128 + cols[0]:k * 128 + cols[0] + 1]
                    dst = cols[1](k)
                if k % 2 == 0:
                    dve(lambda e, src=src, dst=dst, k=k: e.tensor_scalar(
                        out=dst, in0=src, scalar1=dv[:, sc_i, k:k + 1], scalar2=dv[:, sh_i, k:k + 1],
                        op0=ALU.mult, op1=ALU.add),
                        [("bk", tpb), ("dv", sc_i), ("dv", sh_i)], [hT_key])
                else:
                    act(lambda e, src=src, dst=dst, k=k: e.activation(
                        out=dst, in_=src, func=AF.Identity, scale=dv[:, sc_i, k:k + 1], bias=dv[:, sh_i, k:k + 1]),
                        [("bk", tpb), ("dv", sc_i), ("dv", sh_i)], [hT_key])

        def load_weight_k(dst, src, ncols, key):
            for k in range(8):
                gq(lambda e, k=k: e.dma_start(out=dst[:, k, :], in_=src[k * 128:(k + 1) * 128, :]), [], [key])

        A.mark()
        wf32 = A.alloc([128, 8, 512], F32)
        wfu = A.alloc([128, 8, 512], BF16)
        wf = A.alloc([128, 8, 512], BF16)
        shb = A.alloc([128, 8], BF16)
        for k in range(8):
            sp(lambda e, k=k: e.dma_start(out=wf32[:, k, :], in_=w_f[k * 128:(k + 1) * 128, :]), [], [("wf32", k)])
            dve(lambda e, k=k: e.tensor_scalar(out=wf[:, k, :], in0=wf32[:, k, :], scalar1=dv[:, SC1, k:k + 1],
                                               scalar2=None, op0=ALU.mult), [("wf32", k), ("dv", SC1)], ["wf"])
            act(lambda e, k=k: e.copy(out=wfu[:, k, :], in_=wf32[:, k, :]), [("wf32", k)], ["wfu"])
        dve(lambda e: e.tensor_copy(out=shb, in_=dv[:, SH1, :]), [("dv", SH1)], ["shb"])
        biasT = A.alloc([128, 4], F32)
        for g in range(4):
            for k in range(8):
                pe(lambda e, k=k, g=g: e.matmul(bank(7)[:, g:g + 1], lhsT=wfu[:, k, g * 128:(g + 1) * 128],
                                                rhs=shb[:, k:k + 1], start=(k == 0), stop=(k == 7)),
                   ["shb", "wfu"], [("bk", 7)])
        dve(lambda e: e.tensor_copy(out=biasT, in_=bank(7)[:, 0:4]), [("bk", 7)], ["biasT"])
        xtF = Ring("xtF", [A.alloc([128, 1024], F32) for _ in range(4)])
        xnF = Ring("xnF", [A.alloc([128, 1024], BF16) for _ in range(2)])
        hTF = Ring("hTF", [A.alloc([128, 8, 128], BF16) for _ in range(3)])
        FsbR = Ring("Fsb", [A.alloc([128, 512], BF16) for _ in range(3)])
        itemsF = [(x_full[blk * 128:(blk + 1) * 128, :], SC1, SH1) for blk in range(NBLK)]
        wmB = A.alloc([128, 8, 4096], BF16)
        diagB = A.alloc([128, 2, 128], F32)
        for k in range(8):
            gq(lambda e, k=k: e.dma_start(out=wmB[:, k, :], in_=w_mod[k * 128:(k + 1) * 128, 2048:6144]),
               [("wf32", k_) for k_ in range(8)], [("wmB", k)])

        def f_c0(i):
            fb = 2 + (i % 2)
            for g in range(4):
                for k in range(8):
                    pe(lambda e, k=k, g=g: e.matmul(bank(fb)[:, g * 128:(g + 1) * 128],
                                                    lhsT=wf[:, k, g * 128:(g + 1) * 128], rhs=hTF(i)[:, k, :],
                                                    start=(k == 0), stop=(k == 7)), [hTF.key(i), "wf"], [("bk", fb)])

        def f_c1(i):
            fb = 2 + (i % 2)
            dve(lambda e: e.tensor_tensor(out=FsbR(i).rearrange("p (g b) -> p g b", b=128),
                                          in0=bank(fb).rearrange("p (g b) -> p g b", b=128),
                                          in1=biasT.unsqueeze(2).broadcast_to([128, 4, 128]), op=ALU.add),
                [("bk", fb), "biasT"], [FsbR.key(i)])

        def f_c2(i):
            sp(lambda e: e.dma_start(out=F_dram[i].rearrange("g c b -> c g b"),
                                     in_=FsbR(i).rearrange("p (g b) -> p g b", b=128)), [FsbR.key(i)], ["F_dram"])

        run_pipeline(NBLK, norm_stages(itemsF, xtF, xnF, hTF, (0, 1), 0, act_share=0, plain=True) + [f_c0, f_c1, f_c2])
        adaln_group_b(wmB, diagB)
        if debug:
            sp(lambda e: e.dma_start(out=d_x63, in_=xtF(63)), [xtF.key(63)], ["d_x63"])
            sp(lambda e: e.dma_start(out=d_xn63, in_=xnF(63)), [xnF.key(63)], ["d_xn63"])
            sp(lambda e: e.dma_start(out=d_hT63, in_=hTF(63).rearrange("p a b -> p (a b)")), [hTF.key(63)], ["d_hT63"])
        A.release()
        S_.barrier()

        wWa = A.carve_top([128, 8, GAOFF], BF16)
        for k in range(8):
            gq(lambda e, k=k: e.dma_start(out=wWa[:, k, :], in_=w_W[k * 128:(k + 1) * 128, 0:GAOFF]), [], ["wWa"])
        A.mark()
        W1 = A.alloc([128, 128], BF16)
        W2 = A.alloc([128, 2, 128], BF16)
        W3 = A.alloc([128, 64, 2, 72], BF16)
        sp(lambda e: e.dma_start(out=W1, in_=t_w1), [], ["W1"])
        sp(lambda e: e.dma_start(out=W2, in_=t_w2), [], ["W2"])
        sp(lambda e: e.dma_start(out=W3, in_=t_w3), [], ["W3"])
        F2 = A.alloc([128, 128, 128], BF16)
        Y = A.alloc([128, 2, 128, 128], BF16)
        Vr = A.alloc([128, 2, 36, 64], BF16)
        Vi = A.alloc([128, 2, 36, 64], BF16)
        zc = [0]
        rc_ = [0]
        for gp in range(2):
            for gsel in range(2):
                g = 2 * gp + gsel
                for q4 in range(4):
                    sp(lambda e, g=g, gsel=gsel, q4=q4: e.dma_start(
                        out=F2[gsel * 64:(gsel + 1) * 64, q4 * 32:(q4 + 1) * 32, :],
                        in_=F_dram[:, g, q4 * 32:(q4 + 1) * 32, :]), ["F_dram"], [("F2", gsel)])
            n1 = 0
            for gsel in range(2):
                for c16 in range(8):
                    grp = n1 % 2
                    n1 += 1
                    for ci in range(16):
                        c_ = c16 * 16 + ci
                        bk = 4 * grp + ci // 4
                        pe(lambda e, gsel=gsel, c_=c_, ci=ci, bk=bk: e.matmul(
                            bank(bk)[:, (ci % 4) * 128:(ci % 4 + 1) * 128],
                            lhsT=F2[gsel * 64:(gsel + 1) * 64, c_, :], rhs=W1[gsel * 64:(gsel + 1) * 64, :],
                            start=True, stop=True), [("F2", gsel), "W1"], [("bk", bk)])
                    src = psall[:, grp * 2048:(grp + 1) * 2048].rearrange("p (c r) -> p r c", r=128)
                    dst = Y[:, gsel, :, c16 * 16:(c16 + 1) * 16]
                    rk = [("bk", 4 * grp + q_) for q_ in range(4)]
                    if grp == 1:
                        act(lambda e, src=src, dst=dst: e.copy(out=dst, in_=src), rk, [("Y", gsel)])
                    else:
                        dve(lambda e, src=src, dst=dst: e.tensor_copy(out=dst, in_=src), rk, [("Y", gsel)])
            for gsel in range(2):
                g = 2 * gp + gsel
                slot = gsel
                for bt in range(10):
                    a0 = bt * 7
                    na = min(7, 64 - a0)
                    zb = (2, 3, 6)[zc[0] % 3]
                    zc[0] += 1
                    for ai in range(na):
                        al = a0 + ai
                        pe(lambda e, gsel=gsel, al=al, ai=ai, zb=zb: e.matmul(
                            bank(zb)[:, ai * 72:(ai + 1) * 72], lhsT=Y[:, gsel, al, :], rhs=W3[:, al, 0, :],
                            start=True, stop=False), [("Y", gsel), "W3"], [("bk", zb)])
                        pe(lambda e, gsel=gsel, al=al, ai=ai, zb=zb: e.matmul(
                            bank(zb)[:, ai * 72:(ai + 1) * 72], lhsT=Y[:, gsel, 64 + al, :], rhs=W3[:, al, 1, :],
                            start=False, stop=True), [("Y", gsel), "W3"], [("bk", zb)])
                    bz = bank(zb)[:, 0:na * 72].rearrange("p (a x) -> p a x", x=72)
                    re_in = bz[:, :, 0:36].rearrange("p a b -> p b a")
                    im_in = bz[:, :, 36:72].rearrange("p a b -> p b a")
                    re_out = Vr[:, slot, :, a0:a0 + na]
                    im_out = Vi[:, slot, :, a0:a0 + na]
                    if zc[0] % 2 == 0:
                        act(lambda e, re_in=re_in, re_out=re_out: e.copy(out=re_out, in_=re_in), [("bk", zb)], [("Vr", slot)])
                        act(lambda e, im_in=im_in, im_out=im_out: e.copy(out=im_out, in_=im_in), [("bk", zb)], [("Vi", slot)])
                    else:
                        dve(lambda e, re_in=re_in, re_out=re_out: e.tensor_copy(out=re_out, in_=re_in), [("bk", zb)], [("Vr", slot)])
                        dve(lambda e, im_in=im_in, im_out=im_out: e.tensor_copy(out=im_out, in_=im_in), [("bk", zb)], [("Vi", slot)])
                for ch in range(5):
                    nb_ = 8 if ch < 4 else 4
                    cols = nb_ * 64
                    rb = 4 + (rc_[0] % 2)
                    rc_[0] += 1
                    vr = Vr[:, slot, 8 * ch:8 * ch + nb_, :].rearrange("p a b -> p (a b)")
                    vi = Vi[:, slot, 8 * ch:8 * ch + nb_, :].rearrange("p a b -> p (a b)")
                    pe(lambda e, rb=rb, cols=cols, vr=vr: e.matmul(bank(rb)[:, 0:cols], lhsT=W2[:, 0, :], rhs=vr,
                                                                   start=True, stop=False),
                       [("Vr", slot), "W2"], [("bk", rb)])
                    pe(lambda e, rb=rb, cols=cols, vi=vi: e.matmul(bank(rb)[:, 0:cols], lhsT=W2[:, 1, :], rhs=vi,
                                                                   start=False, stop=True),
                       [("Vi", slot), "W2"], [("bk", rb)])
                    dstR = Rt[:, g, 8 * ch * 64:8 * ch * 64 + cols]
                    if rc_[0] % 2 == 0:
                        act(lambda e, rb=rb, cols=cols, dstR=dstR: e.copy(out=dstR, in_=bank(rb)[:, 0:cols]), [("bk", rb)], ["R"])
                    else:
                        dve(lambda e, rb=rb, cols=cols, dstR=dstR: e.tensor_copy(out=dstR, in_=bank(rb)[:, 0:cols]),
                            [("bk", rb)], ["R"])
        if debug:
            sp(lambda e: e.dma_start(out=d_R, in_=Rt.rearrange("p a b -> p (a b)")), ["R"], ["d_R"])
        A.release()
        S_.barrier()

        A.mark()
        kT = A.alloc([128, NWIN * 128 + L], BF16)
        vA = A.alloc([128, NWIN + 2, 2, 65], BF16)
        qT = A.alloc([128, NQB, 4, 128], BF16)
        dve(lambda e: e.memset(vA, 1.0), [], ["vA"])
        A.mark()
        wWb = A.alloc([128, 8, WCOLS - GAOFF], BF16)
        for k in range(8):
            gq(lambda e, k=k: e.dma_start(out=wWb[:, k, :], in_=w_W[k * 128:(k + 1) * 128, GAOFF:WCOLS]), [], ["wWb"])
        cosT = A.alloc([128, NWIN * 128], F32)
        sinT = A.alloc([128, NWIN * 128], F32)
        sp(lambda e: e.dma_start(out=cosT, in_=t_cos), [], ["cos"])
        sp(lambda e: e.dma_start(out=sinT, in_=t_sin), [], ["sin"])
        xtW = Ring("xtW", [A.alloc([128, 1024], F32) for _ in range(4)])
        xnW = Ring("xnW", [A.alloc([128, 1024], BF16) for _ in range(2)])
        hTW = Ring("hTW", [A.alloc([128, 8, 128], BF16) for _ in range(8)])
        r1k = Ring("r1k", [A.alloc([128, 128], F32) for _ in range(2)])
        r2k = Ring("r2k", [A.alloc([128, 128], F32) for _ in range(2)])
        r1q = Ring("r1q", [A.alloc([128, 512], F32) for _ in range(2)])
        r2q = Ring("r2q", [A.alloc([128, 512], F32) for _ in range(2)])
        sgW = Ring("sgW", [A.alloc([128, 2048], BF16) for _ in range(3)])
        itemsW = [(ctxb[cb * 128:(cb + 1) * 128, :], CSC1, CSH1) for cb in range(2)]
        itemsW += [(x_win[wb * 128:(wb + 1) * 128, :], SC1, SH1) for wb in range(NWIN)]

        def w_info(i):
            if i < 2:
                return dict(ctx=True, kcol=NWIN * 128 + i * 128, vidx=NWIN + i, wb=None, isq=False)
            wb = i - 2
            return dict(ctx=False, kcol=wb * 128, vidx=wb, wb=wb, isq=(1 <= wb <= NQB))

        def w_c0(i):
            inf = w_info(i)
            for k in range(8):
                pe(lambda e, k=k: e.matmul(bank(2)[:, 0:128], lhsT=wWa[:, k, KOFF:KOFF + 128], rhs=hTW(i)[:, k, :],
                                           start=(k == 0), stop=(k == 7)), [hTW.key(i), "wWa"], [("bk", 2)])
            if not inf["ctx"]:
                for k in range(8):
                    pe(lambda e, k=k: e.matmul(bank(2)[:, 128:256], lhsT=wWa[:, k, KPOFF:KPOFF + 128],
                                               rhs=hTW(i)[:, k, :], start=(k == 0), stop=(k == 7)),
                       [hTW.key(i), "wWa"], [("bk", 2)])
            for k in range(8):
                pe(lambda e, k=k: e.matmul(bank(3)[:, 0:128], lhsT=hTW(i)[:, k, :], rhs=wWa[:, k, VOFF:VOFF + 128],
                                           start=(k == 0), stop=(k == 7)), [hTW.key(i), "wWa"], [("bk", 3)])

        def w_c1(i):
            inf = w_info(i)
            kcol = inf["kcol"]
            if inf["ctx"]:
                act(lambda e: e.copy(out=kT[:, kcol:kcol + 128], in_=bank(2)[:, 0:128]), [("bk", 2)], ["kT"])
            else:
                rc = inf["wb"] * 128
                dve(lambda e: e.tensor_tensor(out=r1k(i), in0=bank(2)[:, 0:128], in1=cosT[:, rc:rc + 128], op=ALU.mult),
                    [("bk", 2), "cos"], [r1k.key(i)])
                dve(lambda e: e.tensor_tensor(out=r2k(i), in0=bank(2)[:, 128:256], in1=sinT[:, rc:rc + 128], op=ALU.mult),
                    [("bk", 2), "sin"], [r2k.key(i)])
            vidx = inf["vidx"]
            act(lambda e: e.copy(out=vA[:, vidx, :, 0:64], in_=bank(3)[:, 0:128].rearrange("p (g d) -> p g d", d=64)),
                [("bk", 3)], ["vA"])
            if inf["isq"]:
                for (bkq, off) in ((4, QOFF), (5, QPOFF)):
                    for cq in range(4):
                        for k in range(8):
                            pe(lambda e, k=k, cq=cq, bkq=bkq, off=off: e.matmul(
                                bank(bkq)[:, cq * 128:(cq + 1) * 128],
                                lhsT=wWa[:, k, off + cq * 128:off + (cq + 1) * 128], rhs=hTW(i)[:, k, :],
                                start=(k == 0), stop=(k == 7)), [hTW.key(i), "wWa"], [("bk", bkq)])

        def w_c2(i):
            inf = w_info(i)
            kcol = inf["kcol"]
            if not inf["ctx"]:
                pool(lambda e: e.tensor_tensor(out=kT[:, kcol:kcol + 128], in0=r1k(i), in1=r2k(i), op=ALU.add),
                     [r1k.key(i), r2k.key(i)], ["kT"])
            if inf["isq"]:
                wb = inf["wb"]
                cs4 = cosT[:, wb * 128:(wb + 1) * 128].unsqueeze(1).broadcast_to([128, 4, 128])
                sn4 = sinT[:, wb * 128:(wb + 1) * 128].unsqueeze(1).broadcast_to([128, 4, 128])
                dve(lambda e: e.tensor_tensor(out=r1q(i).rearrange("p (c t) -> p c t", t=128),
                                              in0=bank(4).rearrange("p (c t) -> p c t", t=128), in1=cs4, op=ALU.mult),
                    [("bk", 4), "cos"], [r1q.key(i)])
                dve(lambda e: e.tensor_tensor(out=r2q(i).rearrange("p (c t) -> p c t", t=128),
                                              in0=bank(5).rearrange("p (c t) -> p c t", t=128), in1=sn4, op=ALU.mult),
                    [("bk", 5), "sin"], [r2q.key(i)])

        def gates_mm(i, gi):
            gb = 6 + (gi % 2)
            for k in range(8):
                pe(lambda e, k=k: e.matmul(bank(gb), lhsT=hTW(i)[:, k, :],
                                           rhs=wWb[:, k, gi * 512:(gi + 1) * 512],
                                           start=(k == 0), stop=(k == 7)), [hTW.key(i), "wWb"], [("bk", gb)])

        def gates_sig(i, gi):
            gb = 6 + (gi % 2)
            act(lambda e: e.activation(out=sgW(i)[:, gi * 512:(gi + 1) * 512], in_=bank(gb), func=AF.Sigmoid),
                [("bk", gb)], [sgW.key(i)])

        def w_c3(i):
            inf = w_info(i)
            if inf["isq"]:
                qi = inf["wb"] - 1
                pool(lambda e: e.tensor_tensor(out=qT[:, qi, :, :].rearrange("p c t -> p (c t)"), in0=r1q(i), in1=r2q(i),
                                               op=ALU.add), [r1q.key(i), r2q.key(i)], ["qT"])
                gates_mm(i, 0)
                gates_mm(i, 1)

        def w_c4(i):
            inf = w_info(i)
            if inf["isq"]:
                qi = inf["wb"] - 1
                gates_sig(i, 0)
                gates_sig(i, 1)
                gates_mm(i, 2)
                gates_mm(i, 3)
                gates_sig(i, 2)
                gates_sig(i, 3)
                sp(lambda e: e.dma_start(out=SG_dram[qi * 128:(qi + 1) * 128, :], in_=sgW(i)), [sgW.key(i)], ["SG_dram"])

        run_pipeline(len(itemsW), norm_stages(itemsW, xtW, xnW, hTW, (0, 1), 0, merge=True) + [w_c0, w_c1, w_c2, w_c3, w_c4])
        if debug:
            sp(lambda e: e.dma_start(out=d_kT, in_=kT), ["kT"], ["d_kT"])
            sp(lambda e: e.dma_start(out=d_qT, in_=qT.rearrange("p a b c -> p (a b c)")), ["qT"], ["d_qT"])
            sp(lambda e: e.dma_start(out=d_vA, in_=vA.rearrange("p a b c -> p (a b c)")), ["vA"], ["d_vA"])
        A.release()
        S_.barrier()

        A.top_lo = A.nbytes
        A.mark()
        wpa = A.alloc([128, 4, 1024], BF16)
        wpf = A.alloc([128, 4, 1024], BF16)
        wout = A.alloc([128, 8, 1024], BF16)
        for (dst, src, key) in ((wpa, w_pa, "wpa"), (wpf, w_pf, "wpf")):
            for k in range(4):
                gq(lambda e, k=k, dst=dst, src=src: e.dma_start(out=dst[:, k, :], in_=src[k * 128:(k + 1) * 128, :]),
                   [], [key])
        load_weight_k(wout, w_out, 1024, "wout")
        mask = A.alloc([128, 4, 512], BF16)
        sp(lambda e: e.dma_start(out=mask, in_=t_mask), [], ["mask"])
        PTr = [[[A.alloc([128, 512], BF16) for _ in range(5)] for _ in range(2)] for _ in range(2)]
        aoR = Ring("ao", [A.alloc([128, 8, 64], BF16) for _ in range(2)])
        aoTR = Ring("aoT", [A.alloc([128, 4, 128], BF16) for _ in range(3)])
        denR = Ring("den", [A.alloc([128, 8], F32) for _ in range(2)])
        sgtR = Ring("sgt", [A.alloc([128, 2048], BF16) for _ in range(3)])
        tA = [A.alloc([128, 512], F32) for _ in range(2)]
        tB = [A.alloc([128, 512], F32) for _ in range(2)]
        mbR = Ring("mb", [A.alloc([128, 1024], BF16) for _ in range(3)])
        mTR = Ring("mT", [A.alloc([128, 8, 128], BF16) for _ in range(2)])
        ygR = Ring("yg", [A.alloc([128, 1024], F32) for _ in range(2)])
        xthR = Ring("xth", [A.alloc([128, 1024], F32) for _ in range(3)])
        x1R_ = Ring("x1s", [A.alloc([128, 1024], F32) for _ in range(3)])
        xn2R = Ring("xn2", [A.alloc([128, 1024], BF16) for _ in range(2)])
        h2bR = Ring("h2b", [A.alloc([128, 8, 128], BF16) for _ in range(2)])
        spc = [0]

        def tq(i):
            return i + 1, i

        def t_att(i, g):
            qb, qi = tq(i)
            for jt in range(5):
                sb = spc[0] % 2
                spc[0] += 1
                kcol = (qb - 1 + jt) * 128 if jt < 3 else NWIN * 128 + (jt - 3) * 128
                mi = None
                if jt == 0:
                    mi = 2 if qb == 2 else 0
                elif jt == 2:
                    mi = 3 if qb == NQB - 1 else 1
                pe(lambda e, kcol=kcol, sb=sb, mi=mi: e.matmul(
                    bank(sb), lhsT=kT[g * 64:(g + 1) * 64, kcol:kcol + 128],
                    rhs=qT[g * 64:(g + 1) * 64, qi, :, :].rearrange("p c t -> p (c t)"),
                    start=True, stop=(mi is None)), ["kT", "qT"], [("bk", sb)])
                if mi is not None:
                    pe(lambda e, sb=sb, mi=mi: e.matmul(bank(sb), lhsT=idb, rhs=mask[:, mi, :], start=False, stop=True),
                       ["idb", "mask"], [("bk", sb)])
                dstp = PTr[i % 2][g][jt]
                act(lambda e, sb=sb, dstp=dstp: e.activation(out=dstp, in_=bank(sb), func=AF.Exp, scale=0.125),
                    [("bk", sb)], [("PT", i % 2, g, jt)])

        def t1_(i):
            t_att(i, 0)

        def t2_(i):
            t_att(i, 1)

        def t_pv(i, g):
            qb, qi = tq(i)
            ob = 2
            for hh in range(4):
                for jt in range(5):
                    vidx = (qb - 1 + jt) if jt < 3 else NWIN + (jt - 3)
                    src = PTr[i % 2][g][jt]
                    pe(lambda e, hh=hh, vidx=vidx, src=src, jt=jt: e.matmul(
                        bank(ob)[:, hh * 65:(hh + 1) * 65], lhsT=src[:, hh * 128:(hh + 1) * 128],
                        rhs=vA[:, vidx, g, :], start=(jt == 0), stop=(jt == 4)),
                       [("PT", i % 2, g, jt), "vA"], [("bk", ob)])
            ov = bank(ob)[:, 0:260].rearrange("p (h d) -> p h d", d=65)
            dn = denR(i)[:, 4 * g:4 * g + 4]
            dve(lambda e: e.tensor_tensor(out=dn, in0=ov[:, :, 64], in1=dv[:, ESINK, 4 * g:4 * g + 4], op=ALU.add),
                [("bk", ob), ("dv", ESINK)], [("den", i % 2, g)])
            dve(lambda e: e.reciprocal(out=dn, in_=dn), [("den", i % 2, g)], [("den", i % 2, g)])
            dve(lambda e: e.tensor_tensor(out=aoR(i)[:, 4 * g:4 * g + 4, :], in0=ov[:, :, 0:64],
                                          in1=dn.unsqueeze(2).broadcast_to([128, 4, 64]), op=ALU.mult),
                [("bk", ob), ("den", i % 2, g)], [("ao", i % 2, g)])

        def t3_(i):
            t_pv(i, 0)

        def t4_(i):
            qb, qi = tq(i)
            sp(lambda e: e.dma_start(out=sgtR(i), in_=SG_dram[qi * 128:(qi + 1) * 128, :]), ["SG_dram"], [sgtR.key(i)])
            t_pv(i, 1)

        def t5_(i):
            aof = aoR(i).rearrange("p h d -> p (h d)")
            for cj in range(4):
                pe(lambda e, cj=cj: e.transpose(out=bankb(4)[:, cj * 128:(cj + 1) * 128],
                                                in_=aof[:, cj * 128:(cj + 1) * 128], identity=idb),
                   [("ao", i % 2, 0), ("ao", i % 2, 1), "idb"], [("bk", 4)])
            act(lambda e: e.copy(out=aoTR(i).rearrange("p c t -> p (c t)"), in_=bankb(4)[:, 0:512]),
                [("bk", 4)], [aoTR.key(i)])

        def t_merge(i, half):
            qb, qi = tq(i)
            for cj in range(4):
                pe(lambda e, cj=cj: e.matmul(bank(5), lhsT=aoTR(i)[:, cj, :], rhs=wpa[:, cj, half * 512:(half + 1) * 512],
                                             start=(cj == 0), stop=(cj == 3)), [aoTR.key(i), "wpa"], [("bk", 5)])
            for g in range(4):
                pe(lambda e, g=g: e.matmul(bank(6), lhsT=Rt[:, g, qi * 128:(qi + 1) * 128],
                                           rhs=wpf[:, g, half * 512:(half + 1) * 512],
                                           start=(g == 0), stop=(g == 3)), ["R", "wpf"], [("bk", 6)])
            hs = slice(half * 512, (half + 1) * 512)
            hs2 = slice(1024 + half * 512, 1024 + (half + 1) * 512)
            dve(lambda e: e.tensor_tensor(out=tA[half], in0=bank(5), in1=sgtR(i)[:, hs], op=ALU.mult),
                [("bk", 5), sgtR.key(i)], [("tA", half)])
            dve(lambda e: e.tensor_tensor(out=tB[half], in0=bank(6), in1=sgtR(i)[:, hs2], op=ALU.mult),
                [("bk", 6), sgtR.key(i)], [("tB", half)])
            pool(lambda e: e.tensor_tensor(out=mbR(i)[:, hs], in0=tA[half], in1=tB[half], op=ALU.add),
                 [("tA", half), ("tB", half)], [("mb", i % 3, half)])

        def t6_(i):
            t_merge(i, 0)

        def t7_(i):
            t_merge(i, 1)

        def t8_(i):
            for k in range(8):
                pe(lambda e, k=k: e.transpose(out=bankb(4)[:, k * 128:(k + 1) * 128],
                                              in_=mbR(i)[:, k * 128:(k + 1) * 128], identity=idb),
                   [("mb", i % 3, k // 4), "idb"], [("bk", 4)])
            act(lambda e: e.copy(out=mTR(i).rearrange("p c t -> p (c t)"), in_=bankb(4)), [("bk", 4)], [mTR.key(i)])

        def ycol(i):
            return 8 + 4 * (i % 4)

        def t9_(i):
            qb, qi = tq(i)
            sp(lambda e: e.dma_start(out=xthR(i), in_=x_win[qb * 128:(qb + 1) * 128, :]), [], [xthR.key(i)])
            c = ycol(i)
            for half in range(2):
                yb = (3, 7)[half]
                hs = slice(half * 512, (half + 1) * 512)
                for k in range(8):
                    pe(lambda e, k=k, yb=yb, half=half: e.matmul(
                        bank(yb), lhsT=mTR(i)[:, k, :], rhs=wout[:, k, half * 512:(half + 1) * 512],
                        start=(k == 0), stop=(k == 7)), [mTR.key(i), "wout"], [("bk", yb)])
                act(lambda e, yb=yb, half=half: e.activation(out=junk[:, 0:512], in_=bank(yb), func=AF.Square,
                                                             accum_out=small[:, c + half:c + half + 1]),
                    [("bk", yb)], [("sm", c + half), "junk"])
                dve(lambda e, yb=yb, hs=hs: e.tensor_tensor(out=ygR(i)[:, hs], in0=bank(yb), in1=gg1[:, hs], op=ALU.mult),
                    [("bk", yb), "gg1", ("sm", c + half)], [("yg", i % 2, half)])

        def t10_(i):
            c = ycol(i)
            rsy = small[:, c + 2:c + 3]
            dve(lambda e: e.tensor_tensor(out=rsy, in0=small[:, c:c + 1], in1=small[:, c + 1:c + 2], op=ALU.add),
                [("sm", c), ("sm", c + 1)], [("sm", c + 2)])
            act(lambda e: e.activation(out=rsy, in_=rsy, func=AF.Sqrt, scale=1.0 / D, bias=eps_t[:, 0:1]),
                [("sm", c + 2), "eps"], [("sm", c + 2)])

        def t11_(i):
            qb, qi = tq(i)
            c = ycol(i)
            rsy = small[:, c + 2:c + 3]
            dve(lambda e: e.reciprocal(out=rsy, in_=rsy), [("sm", c + 2)], [("sm", c + 2)])
            for half in range(2):
                hs = slice(half * 512, (half + 1) * 512)
                dve(lambda e, hs=hs: e.scalar_tensor_tensor(out=x1R_(i)[:, hs], in0=ygR(i)[:, hs], scalar=rsy,
                                                            in1=xthR(i)[:, hs], op0=ALU.mult, op1=ALU.add),
                    [("yg", i % 2, half), ("sm", c + 2), xthR.key(i)], [("x1s", i % 3, half)])
            if 2 <= qb <= NQB - 1:
                sp(lambda e: e.dma_start(out=X1_dram[(qb - 2) * 128:(qb - 1) * 128, :], in_=x1R_(i)),
                   [("x1s", i % 3, 0), ("x1s", i % 3, 1)], ["X1_dram"])

        def t12_(i):
            c = ycol(i)
            act(lambda e: e.activation(out=junk, in_=x1R_(i), func=AF.Square, accum_out=small[:, c + 3:c + 4]),
                [("x1s", i % 3, 0), ("x1s", i % 3, 1)], [("sm", c + 3), "junk"])
            act(lambda e: e.activation(out=small[:, c + 3:c + 4], in_=small[:, c + 3:c + 4], func=AF.Sqrt,
                                       scale=1.0 / D, bias=eps_t[:, 0:1]), [("sm", c + 3), "eps"], [("sm", c + 3)])

        def t13_(i):
            c = ycol(i)
            dve(lambda e: e.reciprocal(out=small[:, c + 3:c + 4], in_=small[:, c + 3:c + 4]), [("sm", c + 3)], [("sm", c + 3)])
            act(lambda e: e.activation(out=xn2R(i), in_=x1R_(i), func=AF.Copy, scale=small[:, c + 3:c + 4]),
                [("x1s", i % 3, 0), ("x1s", i % 3, 1), ("sm", c + 3)], [xn2R.key(i)])

        def t14_(i):
            qb, qi = tq(i)
            own = 2 <= qb <= NQB - 1
            sci, shi = (SC2, SH2) if own else ((SC2L, SH2L) if qb == 1 else (SC2R, SH2R))
            for k in range(8):
                pe(lambda e, k=k: e.transpose(out=bankb(4)[:, k * 128:(k + 1) * 128],
                                              in_=xn2R(i)[:, k * 128:(k + 1) * 128], identity=idb),
                   [xn2R.key(i), "idb"], [("bk", 4)])
            for k in range(8):
                src = bankb(4)[:, k * 128:(k + 1) * 128]
                dst = h2bR(i)[:, k, :]
                if i % 2 == 0:
                    dve(lambda e, src=src, dst=dst, k=k: e.tensor_scalar(
                        out=dst, in0=src, scalar1=dv[:, sci, k:k + 1], scalar2=dv[:, shi, k:k + 1],
                        op0=ALU.mult, op1=ALU.add), [("bk", 4), ("dv", sci), ("dv", shi)], [h2bR.key(i)])
                else:
                    act(lambda e, src=src, dst=dst, k=k: e.activation(
                        out=dst, in_=src, func=AF.Identity, scale=dv[:, sci, k:k + 1], bias=dv[:, shi, k:k + 1]),
                        [("bk", 4), ("dv", sci), ("dv", shi)], [h2bR.key(i)])
            if own:
                c0 = (qb - 2) * 128
                sp(lambda e: e.dma_start(out=H2_dram[:, :, c0:c0 + 128], in_=h2bR(i)), [h2bR.key(i)], ["H2_dram"])
            elif qb == 1:
                dve(lambda e: e.tensor_copy(out=h2halo[:, :, 0:1], in_=h2bR(i)[:, :, 127:128]), [h2bR.key(i)], ["h2halo"])
            else:
                dve(lambda e: e.tensor_copy(out=h2halo[:, :, 1:2], in_=h2bR(i)[:, :, 0:1]), [h2bR.key(i)], ["h2halo"])

        def tA_(i):
            t1_(i)
            t2_(i)

        def tB_(i):
            t3_(i)
            t4_(i)

        def tC_(i):
            t10_(i)
            t11_(i)

        def tD_(i):
            t12_(i)
            t13_(i)

        def tM_(i):
            t6_(i)
            t7_(i)

        run_pipeline(NQB, [tA_, tB_, t5_, tM_, t8_, t9_, tC_, tD_, t14_])
        A.release()
        A.release()
        S_.barrier()

        A.mark()
        h2T = A.alloc([128, 8, 2050], BF16)
        sp(lambda e: e.dma_start(out=h2T[:, :, 1:2049], in_=H2_dram), ["H2_dram"], ["h2T"])
        dve(lambda e: e.tensor_copy(out=h2T[:, :, 0:1], in_=h2halo[:, :, 0:1]), ["h2halo"], ["h2T"])
        dve(lambda e: e.tensor_copy(out=h2T[:, :, 2049:2050], in_=h2halo[:, :, 1:2]), ["h2halo"], ["h2T"])
        x1c = [A.alloc([128, 1024], F32) for _ in range(4)]
        actb = A.alloc([128, NJ, 1024], BF16)
        wuR = Ring("wu", [A.alloc([128, 8, 128], BF16) for _ in range(3)])
        wgR = Ring("wg", [A.alloc([128, 8, 128], BF16) for _ in range(4)])
        wdR = Ring("wd", [A.alloc([128, 512], BF16) for _ in range(6)])
        ysb = A.alloc([128, 8, 512], F32)
        ysb2 = A.alloc([128, 8, 512], F32)
        ubR = Ring("ub", [A.alloc([128, 514], F32) for _ in range(4)])
        c1R = Ring("c1", [A.alloc([128, 512], F32) for _ in range(2)])
        c3R = Ring("c3", [A.alloc([128, 512], F32) for _ in range(2)])
        silR = Ring("sil", [A.alloc([128, 512], F32) for _ in range(2)])
        x1R = Ring("x1c", x1c)
        CW, CB = 96, 162
        nD = 0
        nO = 0
        for hf in range(2):
            def cinfo(n, hf=hf):
                j, st = n // 2, n % 2
                return j, st, 1 + hf * 1024 + st * 512

            def u0(n, hf=hf):
                j, st, c0 = cinfo(n)
                if st == 0:
                    gq(lambda e: e.dma_start(out=wuR(j), in_=w_upr[j, 0]), [], [wuR.key(j)])
                    gq(lambda e: e.dma_start(out=wgR(j), in_=w_upr[j, 1]), [], [wgR.key(j)])

            def u1(n, hf=hf):
                j, st, c0 = cinfo(n)
                ub, hb = (0, 4) if n % 2 == 0 else (1, 5)
                for k in range(8):
                    pe(lambda e, k=k: e.matmul(bank(ub), lhsT=wuR(j)[:, k, :], rhs=h2T[:, k, c0:c0 + 512],
                                               start=(k == 0), stop=(k == 7)), [wuR.key(j), "h2T"], [("bk", ub)])
                for k in range(8):
                    pe(lambda e, k=k: e.matmul(bank(hb)[:, 0:2], lhsT=wuR(j)[:, k, :],
                                               rhs=h2T[:, k, c0 - 1:c0 + 513:513],
                                               start=(k == 0), stop=(k == 7)), [wuR.key(j), "h2T"], [("bk", hb)])

            def u2(n, hf=hf):
                j, st, c0 = cinfo(n)
                ub, hb = (0, 4) if n % 2 == 0 else (1, 5)
                act(lambda e: e.copy(out=ubR(n)[:, 1:513], in_=bank(ub)), [("bk", ub)], [ubR.key(n)])
                dve(lambda e: e.tensor_copy(out=ubR(n)[:, 0:514:513], in_=bank(hb)[:, 0:2]), [("bk", hb)], [ubR.key(n)])

            def u3(n, hf=hf):
                j, st, c0 = cinfo(n)
                act(lambda e: e.activation(out=c1R(n), in_=ubR(n)[:, 1:513], func=AF.Identity,
                                           scale=vec[:, CW + 22 + j:CW + 23 + j], bias=vec[:, CB + j:CB + j + 1]),
                    [ubR.key(n), "vec"], [c1R.key(n)])

            def u4(n, hf=hf):
                j, st, c0 = cinfo(n)
                dve(lambda e: e.scalar_tensor_tensor(out=c1R(n), in0=ubR(n)[:, 0:512], scalar=vec[:, CW + j:CW + j + 1],
                                                     in1=c1R(n), op0=ALU.mult, op1=ALU.add),
                    [ubR.key(n), c1R.key(n), "vec"], [c1R.key(n)])
                dve(lambda e: e.scalar_tensor_tensor(out=c3R(n), in0=ubR(n)[:, 2:514],
                                                     scalar=vec[:, CW + 44 + j:CW + 45 + j], in1=c1R(n),
                                                     op0=ALU.mult, op1=ALU.add),
                    [ubR.key(n), c1R.key(n), "vec"], [c3R.key(n)])

            def u5(n, hf=hf):
                j, st, c0 = cinfo(n)
                gb = 2 + (n % 2)
                act(lambda e: e.activation(out=silR(n), in_=c3R(n), func=AF.Silu), [c3R.key(n)], [silR.key(n)])
                for k in range(8):
                    pe(lambda e, k=k: e.matmul(bank(gb), lhsT=wgR(j)[:, k, :], rhs=h2T[:, k, c0:c0 + 512],
                                               start=(k == 0), stop=(k == 7)), [wgR.key(j), "h2T"], [("bk", gb)])

            def u6(n, hf=hf):
                j, st, c0 = cinfo(n)
                gb = 2 + (n % 2)
                dve(lambda e: e.tensor_tensor(out=actb[:, j, st * 512:(st + 1) * 512], in0=silR(n), in1=bank(gb),
                                              op=ALU.mult), [silR.key(n), ("bk", gb)], [("actb", j)])

            run_pipeline(2 * NJ, [u0, u1, u2, u3, u4, u5, u6])

            for dh in range(2):
                for j in range(NJ):
                    dsl = nD
                    nD += 1
                    gq(lambda e, j=j, dsl=dsl, dh=dh: e.dma_start(
                        out=wdR(dsl), in_=w_down[j * 128:(j + 1) * 128, dh * 512:(dh + 1) * 512]), [], [wdR.key(dsl)])
                    for tt in range(8):
                        pe(lambda e, j=j, dsl=dsl, tt=tt: e.matmul(
                            bank(tt), lhsT=actb[:, j, tt * 128:(tt + 1) * 128], rhs=wdR(dsl),
                            start=(j == 0), stop=(j == NJ - 1)), [("actb", j), wdR.key(dsl)], [("bk", tt)])
                ydst = ysb if dh == 0 else ysb2
                for tt in range(8):
                    act(lambda e, tt=tt, dh=dh: e.activation(out=junk[:, 0:512], in_=bank(tt), func=AF.Square,
                                                             accum_out=small[:, 16 + 8 * dh + tt:17 + 8 * dh + tt]),
                        [("bk", tt)], [("sm", 16 + 8 * dh + tt), "junk"])
                    dve(lambda e, tt=tt, dh=dh, ydst=ydst: e.tensor_tensor(
                        out=ydst[:, tt, :], in0=bank(tt), in1=gg2[:, dh * 512:(dh + 1) * 512], op=ALU.mult),
                        [("bk", tt), "gg2"], [("ysb", dh, tt)])

            def q0(tt, hf=hf):
                tile_i = hf * 8 + tt
                sp(lambda e: e.dma_start(out=x1R(tile_i), in_=X1_dram[tile_i * 128:(tile_i + 1) * 128, :]),
                   ["X1_dram"], [x1R.key(tile_i)])

            def q1(tt, hf=hf):
                rf = small[:, 32 + tt:33 + tt]
                dve(lambda e: e.tensor_tensor(out=rf, in0=small[:, 16 + tt:17 + tt], in1=small[:, 24 + tt:25 + tt], op=ALU.add),
                    [("sm", 16 + tt), ("sm", 24 + tt)], [("sm", 32 + tt)])
                act(lambda e: e.activation(out=rf, in_=rf, func=AF.Sqrt, scale=1.0 / D, bias=eps_t[:, 0:1]),
                    [("sm", 32 + tt), "eps"], [("sm", 32 + tt)])

            def q2(tt, hf=hf):
                tile_i = hf * 8 + tt
                rf = small[:, 32 + tt:33 + tt]
                dve(lambda e: e.reciprocal(out=rf, in_=rf), [("sm", 32 + tt)], [("sm", 32 + tt)])
                for dh_, ysrc in ((0, ysb), (1, ysb2)):
                    hs = slice(dh_ * 512, (dh_ + 1) * 512)
                    dve(lambda e, hs=hs, ysrc=ysrc: e.scalar_tensor_tensor(
                        out=x1R(tile_i)[:, hs], in0=ysrc[:, tt, :], scalar=rf, in1=x1R(tile_i)[:, hs],
                        op0=ALU.mult, op1=ALU.add),
                        [("ysb", dh_, tt), ("sm", 32 + tt), x1R.key(tile_i)], [x1R.key(tile_i)])

            def q3(tt, hf=hf):
                tile_i = hf * 8 + tt
                sp(lambda e: e.dma_start(out=out[tile_i * 128:(tile_i + 1) * 128, :], in_=x1R(tile_i)),
                   [x1R.key(tile_i)], [("out", tile_i)])

            run_pipeline(8, [q0, q1, q2, q3])
        if debug:
            sp(lambda e: e.dma_start(out=d_hl, in_=h2halo.rearrange("p a b -> p (a b)")), ["h2halo"], ["d_hl"])
        S_.barrier()
        S_.emit(nc, block, sems)
    return nc


def _bf(a):
    return np.ascontiguousarray(a.astype(np.float32)).astype(ml_dtypes.bfloat16)


def _partner(d):
    return d + 16 if (d % 32) < 16 else d - 16


def _const_tables():
    a = np.arange(64, dtype=np.float64)
    al = np.arange(64, dtype=np.float64)
    ang = 2 * np.pi * np.outer(a, al) / 64.0
    w1h = np.concatenate([np.cos(ang), -np.sin(ang)], axis=1)
    w1 = np.concatenate([w1h, w1h], axis=0)
    c = np.arange(128, dtype=np.float64)
    ang = 2 * np.pi * np.outer(c, c) / 128.0
    C, Sn = np.cos(ang), np.sin(ang)
    w2 = np.stack([C, Sn], axis=1)
    i = np.arange(128)
    mL = np.where(i[:, None] >= i[None, :], 0.0, NEGM)
    mR = np.where(i[:, None] <= i[None, :], 0.0, NEGM)
    mN = np.full((128, 128), NEGM)
    return w1, w2, mL, mR, mN


def _core_tables(r):
    n0 = 16 * r
    b = np.arange(128, dtype=np.float64)[:, None, None]
    al = np.arange(64, dtype=np.float64)[None, :, None]
    j = np.arange(36)
    beta = ((2 * (n0 - 1) + j) % 128).astype(np.float64)[None, None, :]
    th = 2 * np.pi * b * (al + 64.0 * beta) / 8192.0
    w3 = np.stack([np.concatenate([np.cos(th), -np.sin(th)], axis=2),
                   np.concatenate([np.sin(th), np.cos(th)], axis=2)], axis=2) / 1024.0
    t0 = 2048 * r
    t = np.arange(NWIN * 128) + t0 - 256
    row = (t // 64).astype(np.float32)
    col = (t % 64).astype(np.float32)
    inv_freq = (np.float32(10000.0) ** (-np.arange(0, 32, 2, dtype=np.float32) / np.float32(32))).astype(np.float32)
    cosT = np.zeros((64, NWIN * 128), np.float32)
    sinT = np.zeros((64, NWIN * 128), np.float32)
    for d in range(64):
        pos = row if d < 32 else col
        angf = (pos * inv_freq[d % 16]).astype(np.float32)
        cosT[d] = np.cos(angf).astype(np.float32)
        sg = -1.0 if (d % 32) < 16 else 1.0
        sinT[d] = (sg * np.sin(angf)).astype(np.float32)
    cosT = np.concatenate([cosT, cosT], axis=0)
    sinT = np.concatenate([sinT, sinT], axis=0)
    return w3, cosT, sinT


_NC_CACHE = {}


def kernel(x, c, ctx, c_ctx, w_mod, b_mod, g_pre1, g_post1, g_pre2, g_post2,
           w_in, sink, w_pa, w_pf, w_out, w_up, conv_w, conv_b, w_down):
    f32 = np.float32
    x = np.asarray(x, f32)
    c = np.asarray(c, f32)
    ctx = np.asarray(ctx, f32)
    c_ctx = np.asarray(c_ctx, f32)
    w_in0 = np.asarray(w_in, f32)[0]
    w_up0 = np.asarray(w_up, f32)[0]

    qcols, qpcols = [], []
    for cq in range(4):
        for h in (cq, 4 + cq):
            for d in range(64):
                qcols.append(h * 64 + d)
                qpcols.append(h * 64 + _partner(d))
    kcols = [512 + g * 64 + d for g in range(2) for d in range(64)]
    kpcols = [512 + g * 64 + _partner(d) for g in range(2) for d in range(64)]
    vcols = list(range(640, 768))
    gcols = list(range(1280, 3328))
    w_W = np.ascontiguousarray(w_in0[:, qcols + qpcols + kcols + kpcols + vcols + gcols])
    assert w_W.shape[1] == WCOLS
    w_f = np.ascontiguousarray(w_in0[:, 768:1280])
    wu = w_up0.reshape(8, 128, 2, NJ, 128)
    w_upr = np.ascontiguousarray(wu.transpose(3, 2, 1, 0, 4))

    w1, w2, mL, mR, mN = _const_tables()
    idb = _bf(np.eye(128))
    idf = np.eye(128, dtype=f32)

    def tile4(m):
        return np.tile(m, (1, 4))

    in_maps = []
    for core in range(8):
        b, r = core // 4, core % 4
        t0 = 2048 * r
        xw = np.zeros((NWIN * 128, D), f32)
        lo, hi = t0 - 256, t0 + 2048 + 256
        slo, shi = max(lo, 0), min(hi, S)
        xw[slo - lo:shi - lo] = x[b, slo:shi]
        vecs = np.zeros((128, NVEC), f32)
        ct = c[b].reshape(8, 128).T
        cct = c_ctx.reshape(8, 128).T
        vecs[:, 0:16] = np.stack([ct, cct], axis=2).reshape(128, 16)
        vecs[:, 16:64] = np.asarray(b_mod, f32)[0].reshape(48, 128).T
        vecs[:, 64:72] = np.asarray(g_pre1, f32)[0].reshape(8, 128).T
        vecs[:, 72:80] = np.asarray(g_post1, f32)[0].reshape(8, 128).T
        vecs[:, 80:88] = np.asarray(g_pre2, f32)[0].reshape(8, 128).T
        vecs[:, 88:96] = np.asarray(g_post2, f32)[0].reshape(8, 128).T
        cw = np.asarray(conv_w, f32)[0]
        vecs[:, 96:162] = cw.reshape(3, NJ, 128).transpose(2, 0, 1).reshape(128, 66)
        vecs[:, 162:184] = np.asarray(conv_b, f32)[0].reshape(NJ, 128).T
        vecs[:, 184:192] = np.broadcast_to(np.asarray(sink, f32)[0][None, :], (128, 8))
        vecs[:, 192] = 1.0 if r > 0 else 0.0
        vecs[:, 193] = 1.0 if r < 3 else 0.0
        w3, cosT, sinT = _core_tables(r)
        masks = np.stack([tile4(mL), tile4(mR), tile4(mN if r == 0 else mL), tile4(mN if r == 3 else mR)], axis=1)
        in_maps.append({
            "x_full": np.ascontiguousarray(x[b]),
            "x_win": xw,
            "ctxb": np.ascontiguousarray(ctx[b]),
            "vecs": vecs,
            "w_mod": np.ascontiguousarray(np.asarray(w_mod, f32)[0]),
            "w_f": w_f,
            "w_W": w_W,
            "w_pa": np.ascontiguousarray(np.asarray(w_pa, f32)[0]),
            "w_pf": np.ascontiguousarray(np.asarray(w_pf, f32)[0]),
            "w_out": np.ascontiguousarray(np.asarray(w_out, f32)[0]),
            "w_upr": w_upr,
            "w_down": np.ascontiguousarray(np.asarray(w_down, f32)[0]),
            "t_w1": _bf(w1),
            "t_w2": _bf(w2),
            "t_w3": _bf(w3),
            "t_cos": cosT,
            "t_sin": sinT,
            "t_mask": _bf(masks),
            "t_idb": idb,
            "t_idf": idf,
        })
    dbgmode = bool(_NC_CACHE.get("debug"))
    key = "nc_dbg" if dbgmode else "nc"
    if key not in _NC_CACHE:
        _NC_CACHE[key] = build_program(debug=dbgmode)
    nc = _NC_CACHE[key]
    res = run_bass_kernel_spmd(nc, in_maps, core_ids=list(range(8)))
    if dbgmode:
        _NC_CACHE["last_results"] = res.results
    outp = np.zeros((2, S, D), f32)
    for core in range(8):
        b, r = core // 4, core % 4
        outp[b, 2048 * r:2048 * (r + 1)] = np.asarray(res.results[core]["out"], f32)
    return outp
```

```python
import math
import numpy as np
import ml_dtypes
import concourse.bass as bass
import concourse.mybir as mybir
from concourse.bass_utils import run_bass_kernel_spmd

F32 = mybir.dt.float32
BF16 = mybir.dt.bfloat16
AF = mybir.ActivationFunctionType
ALU = mybir.AluOpType

D = 1024
S = 8192
NBLK = 64
L = 256
DFF = 2816
NJ = DFF // 128
EPS = 1e-6
NWIN = 20
NQB = 18
WCOLS = 3456
QOFF, QPOFF, KOFF, KPOFF, VOFF, GAOFF, GFOFF = 0, 512, 1024, 1152, 1280, 1408, 2432
NVEC = 200
NEGM = -30000.0

SAME_ENG_SYNC = True
SAME_ENG_RAW_ONLY = True


class Sched:
    COMPUTE = ("pe", "act", "dve", "pool")
    DMAQ = ("sp", "gq")
    RING = 8

    def __init__(self):
        self.ops = []
        self.last_w = {}
        self.readers = {}
        self.per_eng = {e: [] for e in self.COMPUTE + self.DMAQ}

    def op(self, eng, fn, reads=(), writes=(), extra=()):
        oid = len(self.ops)
        deps = set(extra)
        raw = set(extra)
        for k in reads:
            w = self.last_w.get(k)
            if w is not None:
                deps.add(w)
                raw.add(w)
            if isinstance(k, tuple) and k[0] == "bk":
                for r_ in self.readers.get(k, ()):
                    if self.ops[r_]["eng"] != eng:
                        deps.add(r_)
        for k in writes:
            w = self.last_w.get(k)
            if w is not None:
                deps.add(w)
            for r_ in self.readers.get(k, ()):
                deps.add(r_)
        for k in reads:
            self.readers.setdefault(k, []).append(oid)
        for k in writes:
            self.last_w[k] = oid
            self.readers[k] = []
        deps.discard(oid)
        if SAME_ENG_RAW_ONLY and eng in self.COMPUTE:
            deps = {d for d in deps if self.ops[d]["eng"] != eng or d in raw}
        self.ops.append(dict(eng=eng, fn=fn, deps=deps, idx=len(self.per_eng[eng])))
        self.per_eng[eng].append(oid)
        return oid

    def barrier(self):
        tails = []
        for e, lst in self.per_eng.items():
            if e in self.DMAQ:
                tails += lst[-self.RING:]
            elif lst:
                tails.append(lst[-1])
        ids = []
        for e in self.COMPUTE + self.DMAQ:
            ids.append(self.op(e, None, extra=tails))
        return ids

    def emit(self, nc, block, sems):
        ops = self.ops
        for o in ops:
            o["sig"] = False
        for o in ops:
            for d in o["deps"]:
                od = ops[d]
                if od["eng"] == o["eng"] and o["eng"] == "pe":
                    continue
                if od["eng"] == o["eng"] and o["eng"] in self.COMPUTE and not SAME_ENG_SYNC:
                    continue
                od["sig"] = True
        for e in self.COMPUTE:
            cnt = 0
            for oid in self.per_eng[e]:
                if ops[oid]["sig"]:
                    cnt += 1
                    ops[oid]["val"] = cnt
            nxt = None
            for oid in reversed(self.per_eng[e]):
                if ops[oid]["sig"]:
                    nxt = ops[oid]["val"]
                ops[oid]["cval"] = nxt
        for e in self.DMAQ:
            real = [oid for oid in self.per_eng[e] if ops[oid]["fn"] is not None]
            for j, oid in enumerate(real):
                ops[oid]["dsem"] = j % self.RING
                ops[oid]["dval"] = 16 * (j // self.RING + 1)
                ops[oid]["dprev"] = real[j - self.RING] if j >= self.RING else None

        def waits_for(o):
            need = {}
            deps = set(o["deps"])
            if o["eng"] in self.DMAQ and o["fn"] is not None and o.get("dprev") is not None:
                deps.add(o["dprev"])
            for d in deps:
                od = ops[d]
                if od["fn"] is None and od["eng"] in self.DMAQ:
                    continue
                if od["eng"] in self.DMAQ:
                    key = (od["eng"], od["dsem"])
                    val = od["dval"]
                else:
                    if od["eng"] == o["eng"]:
                        if o["eng"] == "pe" or not SAME_ENG_SYNC:
                            continue
                    key = (od["eng"], 0)
                    val = od["cval"]
                    assert val is not None
                if need.get(key, 0) < val:
                    need[key] = val
            return need

        def run(ename, eng_handle):
            known = {}
            for oid in self.per_eng[ename]:
                o = ops[oid]
                need = waits_for(o)
                for key, val in sorted(need.items()):
                    if known.get(key, 0) >= val:
                        continue
                    known[key] = val
                    eng_handle.wait_ge(sems[key], val)
                if o["fn"] is None:
                    if ename in self.COMPUTE and o["sig"]:
                        eng_handle.nop().then_inc(sems[(ename, 0)], 1)
                    continue
                ins = o["fn"](eng_handle)
                if ename in self.DMAQ:
                    ins.then_inc(sems[(ename, o["dsem"])], 16)
                elif o["sig"]:
                    ins.then_inc(sems[(ename, 0)], 1)

        @block.tensor
        def _(e):
            run("pe", e)

        @block.scalar
        def _(e):
            run("act", e)

        @block.vector
        def _(e):
            run("dve", e)

        @block.gpsimd
        def _(e):
            merged = sorted(self.per_eng["pool"] + self.per_eng["gq"])
            known = {}
            for oid in merged:
                o = ops[oid]
                need = waits_for(o)
                for key, val in sorted(need.items()):
                    if known.get(key, 0) >= val:
                        continue
                    known[key] = val
                    e.wait_ge(sems[key], val)
                if o["fn"] is None:
                    if o["eng"] == "pool" and o["sig"]:
                        e.nop().then_inc(sems[("pool", 0)], 1)
                    continue
                ins = o["fn"](e)
                if o["eng"] == "gq":
                    ins.then_inc(sems[("gq", o["dsem"])], 16)
                elif o["sig"]:
                    ins.then_inc(sems[("pool", 0)], 1)

        @block.sync
        def _(e):
            run("sp", e)


class Arena:
    def __init__(self, big, nbytes):
        self.big = big
        self.nbytes = nbytes
        self.off = 0
        self.marks = []

    def alloc(self, shape, dtype):
        esz = 2 if dtype == BF16 else 4
        n = 1
        for s in shape[1:]:
            n *= s
        nb = (n * esz + 63) // 64 * 64
        assert self.off + nb <= getattr(self, "top_lo", self.nbytes), ("arena overflow", self.off, nb, self.nbytes)
        v = self.big[:, self.off // 4:(self.off + nb) // 4]
        self.off += nb
        if dtype == BF16:
            v = v.bitcast(BF16)
        v = v[:, 0:n]
        if len(shape) == 2:
            return v
        if len(shape) == 3:
            return v.rearrange("p (a b) -> p a b", b=shape[2])
        if len(shape) == 4:
            return v.rearrange("p (a b c) -> p a b c", b=shape[2], c=shape[3])
        raise ValueError(shape)

    def carve_top(self, shape, dtype):
        esz = 2 if dtype == BF16 else 4
        n = 1
        for s_ in shape[1:]:
            n *= s_
        nb = (n * esz + 63) // 64 * 64
        lo = self.nbytes - nb
        v = self.big[:, lo // 4:self.nbytes // 4]
        if dtype == BF16:
            v = v.bitcast(BF16)
        v = v[:, 0:n]
        self.top_lo = lo
        return v.rearrange("p (a b) -> p a b", b=shape[2]) if len(shape) == 3 else v

    def mark(self):
        self.marks.append(self.off)

    def release(self):
        self.off = self.marks.pop()


def build_program(debug=False):
    nc = bass.Bass("TRN2", target_bir_lowering=False)

    def din(name, shape, dt=F32):
        return nc.dram_tensor(name, list(shape), dt, kind="ExternalInput").ap()

    x_full = din("x_full", [S, D])
    x_win = din("x_win", [NWIN * 128, D])
    ctxb = din("ctxb", [L, D])
    vecs = din("vecs", [128, NVEC])
    w_mod = din("w_mod", [D, 6 * D])
    w_f = din("w_f", [D, 512])
    w_W = din("w_W", [D, WCOLS])
    w_pa = din("w_pa", [512, D])
    w_pf = din("w_pf", [512, D])
    w_out = din("w_out", [D, D])
    w_upr = din("w_upr", [NJ, 2, 128, 8, 128])
    w_down = din("w_down", [DFF, D])
    t_w1 = din("t_w1", [128, 128], BF16)
    t_w2 = din("t_w2", [128, 2, 128], BF16)
    t_w3 = din("t_w3", [128, 64, 2, 72], BF16)
    t_cos = din("t_cos", [128, NWIN * 128])
    t_sin = din("t_sin", [128, NWIN * 128])
    t_mask = din("t_mask", [128, 4, 512], BF16)
    t_idb = din("t_idb", [128, 128], BF16)
    t_idf = din("t_idf", [128, 128])
    out = nc.dram_tensor("out", [2048, D], F32, kind="ExternalOutput").ap()
    skind = dict(kind="ExternalOutput") if debug else {}
    F_dram = nc.dram_tensor("F_scr", [NBLK, 4, 128, 128], BF16, **skind).ap()
    SG_dram = nc.dram_tensor("SG_scr", [NQB * 128, 2048], BF16, **skind).ap()
    X1_dram = nc.dram_tensor("X1_scr", [2048, D], F32, **skind).ap()
    H2_dram = nc.dram_tensor("H2_scr", [128, 8, 2048], BF16, **skind).ap()
    if debug:
        d_dv = nc.dram_tensor("d_dv", [128, 128], F32, kind="ExternalOutput").ap()
        d_gg1 = nc.dram_tensor("d_gg1", [128, 1024], F32, kind="ExternalOutput").ap()
        d_R = nc.dram_tensor("d_R", [128, 4 * NQB * 128], BF16, kind="ExternalOutput").ap()
        d_kT = nc.dram_tensor("d_kT", [128, NWIN * 128 + L], BF16, kind="ExternalOutput").ap()
        d_qT = nc.dram_tensor("d_qT", [128, NQB * 512], BF16, kind="ExternalOutput").ap()
        d_vA = nc.dram_tensor("d_vA", [128, (NWIN + 2) * 130], BF16, kind="ExternalOutput").ap()
        d_hl = nc.dram_tensor("d_hl", [128, 16], BF16, kind="ExternalOutput").ap()
        d_x63 = nc.dram_tensor("d_x63", [128, 1024], F32, kind="ExternalOutput").ap()
        d_xn63 = nc.dram_tensor("d_xn63", [128, 1024], BF16, kind="ExternalOutput").ap()
        d_hT63 = nc.dram_tensor("d_hT63", [128, 1024], BF16, kind="ExternalOutput").ap()
        d_F63 = nc.dram_tensor("d_F63", [128, 512], BF16, kind="ExternalOutput").ap()

    S_ = Sched()
    ARENA_BYTES = 200 * 1024

    from contextlib import ExitStack
    with ExitStack() as es:
        big = es.enter_context(nc.sbuf_tensor("arena", [128, ARENA_BYTES // 4], F32))
        psall = es.enter_context(nc.psum_tensor("psall", [128, 4096], F32))
        sems = {}
        for e in Sched.COMPUTE:
            sems[(e, 0)] = es.enter_context(nc.semaphore("s_" + e))
        for e in Sched.DMAQ:
            for j in range(Sched.RING):
                sems[(e, j)] = es.enter_context(nc.semaphore("s_%s%d" % (e, j)))
        block = es.enter_context(nc.Block())

        A = Arena(big, ARENA_BYTES)

        def bank(i):
            return psall[:, i * 512:(i + 1) * 512]

        def bankb(i):
            return psall[:, i * 512:(i + 1) * 512].bitcast(BF16)

        vec = A.alloc([128, NVEC], F32)
        idb = A.alloc([128, 128], BF16)
        idf = A.alloc([128, 128], F32)
        onesf = A.alloc([128, 128], F32)
        modT = A.alloc([128, 48, 2], F32)
        dv = A.alloc([128, 16, 8], F32)
        gg1 = A.alloc([128, 1024], F32)
        gg2 = A.alloc([128, 1024], F32)
        Rt = A.alloc([128, 4, NQB * 128], BF16)
        junk = A.alloc([128, 1024], BF16)
        small = A.alloc([128, 64], F32)
        eps_t = A.alloc([128, 1], F32)
        h2halo = A.alloc([128, 8, 2], BF16)
        (SC1, SH1, CSC1, CSH1, SC2, SH2, SC2L, SH2L, SC2R, SH2R, GG1, GG2, ESINK) = range(13)

        def dve(fn, r, w):
            return S_.op("dve", fn, r, w)

        def act(fn, r, w):
            return S_.op("act", fn, r, w)

        def pool(fn, r, w):
            return S_.op("pool", fn, r, w)

        def pe(fn, r, w):
            return S_.op("pe", fn, r, w)

        def sp(fn, r, w):
            return S_.op("sp", fn, r, w)

        def gq(fn, r, w):
            return S_.op("gq", fn, r, w)

        sp(lambda e: e.dma_start(out=vec, in_=vecs), [], ["vec"])
        sp(lambda e: e.dma_start(out=idb, in_=t_idb), [], ["idb"])
        sp(lambda e: e.dma_start(out=idf, in_=t_idf), [], ["idf"])
        dve(lambda e: e.memset(onesf, 1.0), [], ["onesf"])
        dve(lambda e: e.memset(eps_t, EPS), [], ["eps"])

        scb = A.alloc([128, 8, 2], BF16)
        A.mark()
        wmb = [A.alloc([128, 2048], BF16) for _ in range(2)]
        act(lambda e: e.activation(out=scb.rearrange("p a b -> p (a b)"), in_=vec[:, 0:16], func=AF.Silu),
            ["vec"], ["scb"])
        dve(lambda e: e.tensor_copy(out=modT, in_=vec[:, 16:64].unsqueeze(2).broadcast_to([128, 48, 2])),
            ["vec"], ["modT"])
        wst = A.alloc([128, 2048], F32)
        for k in range(8):
            sl = k % 2
            if sl == 0:
                gq(lambda e, k=k, sl=sl: e.dma_start(out=wmb[sl], in_=w_mod[k * 128:(k + 1) * 128, 0:2048]),
                   [], [("wmb", sl)])
            else:
                sp(lambda e, k=k: e.dma_start(out=wst, in_=w_mod[k * 128:(k + 1) * 128, 0:2048]), [], ["wst"])
                dve(lambda e, sl=sl: e.tensor_copy(out=wmb[sl], in_=wst), ["wst"], [("wmb", sl)])
            for m in range(16):
                pe(lambda e, k=k, m=m, sl=sl: e.matmul(
                    bank(sl)[:, 2 * m:2 * m + 2], lhsT=wmb[sl][:, m * 128:(m + 1) * 128], rhs=scb[:, k, :],
                    start=True, stop=True),
                   [("wmb", sl), "scb"], [("bk", sl)])
            dve(lambda e, sl=sl: e.tensor_tensor(
                out=modT[:, 0:16, :], in0=bank(sl)[:, 0:32].rearrange("p (m t) -> p m t", t=2), in1=modT[:, 0:16, :],
                op=ALU.add), [("bk", sl), "modT"], ["modT"])
        G_PRE1, G_POST1, G_PRE2, G_POST2 = 64, 72, 80, 88

        def mod(sec, which):
            return modT[:, sec * 8:(sec + 1) * 8, which]

        def dvs(i):
            return dv[:, i, :]

        def mk_scale(dst, sec, which, goff, mk="modT"):
            dve(lambda e: e.tensor_tensor(out=dvs(dst), in0=mod(sec, which), in1=vec[:, goff:goff + 8], op=ALU.mult),
                [mk, "vec"], [("dv", dst)])
            dve(lambda e: e.tensor_tensor(out=dvs(dst), in0=dvs(dst), in1=vec[:, goff:goff + 8], op=ALU.add),
                [("dv", dst), "vec"], [("dv", dst)])

        mk_scale(SC1, 1, 0, G_PRE1)
        mk_scale(CSC1, 1, 1, G_PRE1)
        dve(lambda e: e.tensor_copy(out=dvs(SH1), in_=mod(0, 0)), ["modT"], [("dv", SH1)])
        dve(lambda e: e.tensor_copy(out=dvs(CSH1), in_=mod(0, 1)), ["modT"], [("dv", CSH1)])
        act(lambda e: e.activation(out=dvs(ESINK), in_=vec[:, 184:192], func=AF.Exp), ["vec"], [("dv", ESINK)])
        A.release()
        S_.barrier()

        def adaln_group_b(wmB, diag):
            for k in range(8):
                sl = 6 + (k % 2)
                for m in range(32):
                    pe(lambda e, k=k, m=m, sl=sl: e.matmul(
                        bank(sl)[:, 2 * m:2 * m + 2], lhsT=wmB[:, k, m * 128:(m + 1) * 128], rhs=scb[:, k, :],
                        start=True, stop=True), [("wmB", k), "scb"], [("bk", sl)])
                dve(lambda e, sl=sl: e.tensor_tensor(
                    out=modT[:, 16:48, :], in0=bank(sl)[:, 0:64].rearrange("p (m t) -> p m t", t=2),
                    in1=modT[:, 16:48, :], op=ALU.add), [("bk", sl), "modTB"], ["modTB"])
            mk_scale(SC2, 4, 0, G_PRE2, mk="modTB")
            dve(lambda e: e.tensor_copy(out=dvs(SH2), in_=mod(3, 0)), ["modTB"], [("dv", SH2)])
            for dst, src, fl in ((SC2L, SC2, 192), (SH2L, SH2, 192), (SC2R, SC2, 193), (SH2R, SH2, 193)):
                dve(lambda e, dst=dst, src=src, fl=fl: e.tensor_scalar(
                    out=dvs(dst), in0=dvs(src), scalar1=vec[:, fl:fl + 1], scalar2=None, op0=ALU.mult),
                    [("dv", src), "vec"], [("dv", dst)])
            dve(lambda e: e.tensor_tensor(out=dvs(GG1), in0=mod(2, 0), in1=vec[:, G_POST1:G_POST1 + 8], op=ALU.mult),
                ["modTB", "vec"], [("dv", GG1)])
            dve(lambda e: e.tensor_tensor(out=dvs(GG2), in0=mod(5, 0), in1=vec[:, G_POST2:G_POST2 + 8], op=ALU.mult),
                ["modTB", "vec"], [("dv", GG2)])
            n_ = 0
            for (src, dstt, nm) in ((GG1, gg1, "gg1"), (GG2, gg2, "gg2")):
                for c in range(8):
                    sl = n_ % 2
                    bk = 6 + (n_ % 2)
                    n_ += 1
                    dve(lambda e, src=src, c=c, sl=sl: e.tensor_scalar(
                        out=diag[:, sl, :], in0=idf, scalar1=dv[:, src, c:c + 1], scalar2=None, op0=ALU.mult),
                        [("dv", src), "idf"], [("diag", sl)])
                    pe(lambda e, sl=sl, bk=bk: e.matmul(bank(bk)[:, 0:128], lhsT=onesf, rhs=diag[:, sl, :],
                                                        start=True, stop=True),
                       [("diag", sl), "onesf"], [("bk", bk)])
                    act(lambda e, dstt=dstt, c=c, bk=bk: e.copy(out=dstt[:, c * 128:(c + 1) * 128],
                                                               in_=bank(bk)[:, 0:128]), [("bk", bk)], [nm])
            if debug:
                sp(lambda e: e.dma_start(out=d_dv, in_=dv.rearrange("p a b -> p (a b)")),
                   [("dv", i) for i in range(13)], ["d_dv"])
                sp(lambda e: e.dma_start(out=d_gg1, in_=gg1), ["gg1"], ["d_gg1"])

        def run_pipeline(nitems, stages):
            ns = len(stages)
            for t in range(nitems + ns - 1):
                for st_ in reversed(range(ns)):
                    i = t - st_
                    if 0 <= i < nitems:
                        stages[st_](i)

        class Ring:
            def __init__(self, name, bufs):
                self.name = name
                self.bufs = bufs

            def __call__(self, i):
                return self.bufs[i % len(self.bufs)]

            def key(self, i):
                return (self.name, i % len(self.bufs))

        def norm_stages(items, xt, xn, hT, tpbanks, smbase, act_share=1, plain=False, merge=False):
            def ssc(i):
                return smbase + 2 * (i % 4)

            def s0(i):
                sp(lambda e: e.dma_start(out=xt(i), in_=items[i][0]), [], [xt.key(i)])

            def s1(i):
                c = ssc(i)
                act(lambda e: e.activation(out=junk, in_=xt(i), func=AF.Square, accum_out=small[:, c:c + 1]),
                    [xt.key(i)], [("sm", c), "junk"])
                act(lambda e: e.activation(out=small[:, c + 1:c + 2], in_=small[:, c:c + 1], func=AF.Sqrt,
                                           scale=1.0 / D, bias=eps_t[:, 0:1]), [("sm", c), "eps"], [("sm", c + 1)])

            def s2(i):
                c = ssc(i)
                dve(lambda e: e.reciprocal(out=small[:, c + 1:c + 2], in_=small[:, c + 1:c + 2]),
                    [("sm", c + 1)], [("sm", c + 1)])

            def s3(i):
                c = ssc(i)
                act(lambda e: e.activation(out=xn(i), in_=xt(i), func=AF.Copy, scale=small[:, c + 1:c + 2]),
                    [xt.key(i), ("sm", c + 1)], [xn.key(i)])

            def s4(i):
                tb = tpbanks[i % len(tpbanks)]
                for k in range(8):
                    pe(lambda e, k=k: e.transpose(out=bankb(tb)[:, k * 128:(k + 1) * 128],
                                                  in_=xn(i)[:, k * 128:(k + 1) * 128], identity=idb),
                       [xn.key(i), "idb"], [("bk", tb)])

            def s5(i):
                tb = tpbanks[i % len(tpbanks)]
                sc_i, sh_i = items[i][1], items[i][2]
                if plain:
                    dve(lambda e: e.tensor_copy(out=hT(i).rearrange("p a b -> p (a b)"), in_=bankb(tb)),
                        [("bk", tb)], [hT.key(i)])
                    return
                for k in range(8):
                    src = bankb(tb)[:, k * 128:(k + 1) * 128]
                    dst = hT(i)[:, k, :]
                    if i % 4 != 3 or act_share == 0:
                        dve(lambda e, src=src, dst=dst, k=k: e.tensor_scalar(
                            out=dst, in0=src, scalar1=dv[:, sc_i, k:k + 1], scalar2=dv[:, sh_i, k:k + 1],
                            op0=ALU.mult, op1=ALU.add),
                            [("bk", tb), ("dv", sc_i), ("dv", sh_i)], [hT.key(i)])
                    else:
                        act(lambda e, src=src, dst=dst, k=k: e.activation(
                            out=dst, in_=src, func=AF.Identity, scale=dv[:, sc_i, k:k + 1], bias=dv[:, sh_i, k:k + 1]),
                            [("bk", tb), ("dv", sc_i), ("dv", sh_i)], [hT.key(i)])

            if merge:
                def s123(i):
                    s1(i)
                    s2(i)
                    s3(i)
                return [s0, s123, s4, s5]
            return [s0, s1, s2, s3, s4, s5]

        def norm_to_hT(tag, xt_ap, xt_key, xn, xn_key, hT_dst, hT_key, sc_i, sh_i, tpb, ssi, cols=None):
            ss = small[:, ssi:ssi + 1]
            rs = small[:, ssi + 1:ssi + 2]
            act(lambda e: e.activation(out=junk, in_=xt_ap, func=AF.Square, accum_out=ss),
                [xt_key], [("sm", ssi)])
            act(lambda e: e.activation(out=rs, in_=ss, func=AF.Sqrt, scale=1.0 / D, bias=eps_t[:, 0:1]),
                [("sm", ssi), "eps"], [("sm", ssi + 1)])
            dve(lambda e: e.reciprocal(out=rs, in_=rs), [("sm", ssi + 1)], [("sm", ssi + 1)])
            act(lambda e: e.activation(out=xn, in_=xt_ap, func=AF.Copy, scale=rs),
                [xt_key, ("sm", ssi + 1)], [xn_key])
            for k in range(8):
                pe(lambda e, k=k: e.transpose(out=bankb(tpb)[:, k * 128:(k + 1) * 128],
                                              in_=xn[:, k * 128:(k + 1) * 128], identity=idb),
                   [xn_key, "idb"], [("bk", tpb)])
            for k in range(8):
                if cols is None:
                    src = bankb(tpb)[:, k * 128:(k + 1) * 128]
                    dst = hT_dst[:, k, :]
                else:
                    src = bankb(tpb)[:, k * 128 + cols[0]:k * 128 + cols[0] + 1]
                    dst = cols[1](k)
                if k % 2 == 0:
                    dve(lambda e, src=src, dst=dst, k=k: e.tensor_scalar(
                        out=dst, in0=src, scalar1=dv[:, sc_i, k:k + 1], scalar2=dv[:, sh_i, k:k + 1],
                        op0=ALU.mult, op1=ALU.add),
                        [("bk", tpb), ("dv", sc_i), ("dv", sh_i)], [hT_key])
                else:
                    act(lambda e, src=src, dst=dst, k=k: e.activation(
                        out=dst, in_=src, func=AF.Identity, scale=dv[:, sc_i, k:k + 1], bias=dv[:, sh_i, k:k + 1]),
                        [("bk", tpb), ("dv", sc_i), ("dv", sh_i)], [hT_key])

        def load_weight_k(dst, src, ncols, key):
            for k in range(8):
                gq(lambda e, k=k: e.dma_start(out=dst[:, k, :], in_=src[k * 128:(k + 1) * 128, :]), [], [key])

        A.mark()
        wf32 = A.alloc([128, 8, 512], F32)
        wfu = A.alloc([128, 8, 512], BF16)
        wf = A.alloc([128, 8, 512], BF16)
        shb = A.alloc([128, 8], BF16)
        for k in range(8):
            sp(lambda e, k=k: e.dma_start(out=wf32[:, k, :], in_=w_f[k * 128:(k + 1) * 128, :]), [], [("wf32", k)])
            dve(lambda e, k=k: e.tensor_scalar(out=wf[:, k, :], in0=wf32[:, k, :], scalar1=dv[:, SC1, k:k + 1],
                                               scalar2=None, op0=ALU.mult), [("wf32", k), ("dv", SC1)], ["wf"])
            act(lambda e, k=k: e.copy(out=wfu[:, k, :], in_=wf32[:, k, :]), [("wf32", k)], ["wfu"])
        dve(lambda e: e.tensor_copy(out=shb, in_=dv[:, SH1, :]), [("dv", SH1)], ["shb"])
        biasT = A.alloc([128, 4], F32)
        for g in range(4):
            for k in range(8):
                pe(lambda e, k=k, g=g: e.matmul(bank(7)[:, g:g + 1], lhsT=wfu[:, k, g * 128:(g + 1) * 128],
                                                rhs=shb[:, k:k + 1], start=(k == 0), stop=(k == 7)),
                   ["shb", "wfu"], [("bk", 7)])
        dve(lambda e: e.tensor_copy(out=biasT, in_=bank(7)[:, 0:4]), [("bk", 7)], ["biasT"])
        xtF = Ring("xtF", [A.alloc([128, 1024], F32) for _ in range(4)])
        xnF = Ring("xnF", [A.alloc([128, 1024], BF16) for _ in range(2)])
        hTF = Ring("hTF", [A.alloc([128, 8, 128], BF16) for _ in range(3)])
        FsbR = Ring("Fsb", [A.alloc([128, 512], BF16) for _ in range(3)])
        itemsF = [(x_full[blk * 128:(blk + 1) * 128, :], SC1, SH1) for blk in range(NBLK)]
        wmB = A.alloc([128, 8, 4096], BF16)
        diagB = A.alloc([128, 2, 128], F32)
        for k in range(8):
            gq(lambda e, k=k: e.dma_start(out=wmB[:, k, :], in_=w_mod[k * 128:(k + 1) * 128, 2048:6144]),
               [], [("wmB", k)])

        def f_c0(i):
            fb = 2 + (i % 2)
            for g in range(4):
                for k in range(8):
                    pe(lambda e, k=k, g=g: e.matmul(bank(fb)[:, g * 128:(g + 1) * 128],
                                                    lhsT=wf[:, k, g * 128:(g + 1) * 128], rhs=hTF(i)[:, k, :],
                                                    start=(k == 0), stop=(k == 7)), [hTF.key(i), "wf"], [("bk", fb)])

        def f_c1(i):
            fb = 2 + (i % 2)
            dve(lambda e: e.tensor_tensor(out=FsbR(i).rearrange("p (g b) -> p g b", b=128),
                                          in0=bank(fb).rearrange("p (g b) -> p g b", b=128),
                                          in1=biasT.unsqueeze(2).broadcast_to([128, 4, 128]), op=ALU.add),
                [("bk", fb), "biasT"], [FsbR.key(i)])

        def f_c2(i):
            sp(lambda e: e.dma_start(out=F_dram[i].rearrange("g c b -> c g b"),
                                     in_=FsbR(i).rearrange("p (g b) -> p g b", b=128)), [FsbR.key(i)], ["F_dram"])

        run_pipeline(NBLK, norm_stages(itemsF, xtF, xnF, hTF, (0, 1), 0, act_share=0, plain=True) + [f_c0, f_c1, f_c2])
        adaln_group_b(wmB, diagB)
        if debug:
            sp(lambda e: e.dma_start(out=d_x63, in_=xtF(63)), [xtF.key(63)], ["d_x63"])
            sp(lambda e: e.dma_start(out=d_xn63, in_=xnF(63)), [xnF.key(63)], ["d_xn63"])
            sp(lambda e: e.dma_start(out=d_hT63, in_=hTF(63).rearrange("p a b -> p (a b)")), [hTF.key(63)], ["d_hT63"])
        A.release()
        S_.barrier()

        wWa = A.carve_top([128, 8, GAOFF], BF16)
        for k in range(8):
            gq(lambda e, k=k: e.dma_start(out=wWa[:, k, :], in_=w_W[k * 128:(k + 1) * 128, 0:GAOFF]), [], ["wWa"])
        A.mark()
        W1 = A.alloc([128, 128], BF16)
        W2 = A.alloc([128, 2, 128], BF16)
        W3 = A.alloc([128, 64, 2, 72], BF16)
        sp(lambda e: e.dma_start(out=W1, in_=t_w1), [], ["W1"])
        sp(lambda e: e.dma_start(out=W2, in_=t_w2), [], ["W2"])
        sp(lambda e: e.dma_start(out=W3, in_=t_w3), [], ["W3"])
        F2 = A.alloc([128, 128, 128], BF16)
        Y = A.alloc([128, 2, 128, 128], BF16)
        Vr = A.alloc([128, 2, 36, 64], BF16)
        Vi = A.alloc([128, 2, 36, 64], BF16)
        zc = [0]
        rc_ = [0]
        for gp in range(2):
            for gsel in range(2):
                g = 2 * gp + gsel
                for q4 in range(4):
                    sp(lambda e, g=g, gsel=gsel, q4=q4: e.dma_start(
                        out=F2[gsel * 64:(gsel + 1) * 64, q4 * 32:(q4 + 1) * 32, :],
                        in_=F_dram[:, g, q4 * 32:(q4 + 1) * 32, :]), ["F_dram"], [("F2", gsel)])
            n1 = 0
            for gsel in range(2):
                for c16 in range(8):
                    grp = n1 % 2
                    n1 += 1
                    for ci in range(16):
                        c_ = c16 * 16 + ci
                        bk = 4 * grp + ci // 4
                        pe(lambda e, gsel=gsel, c_=c_, ci=ci, bk=bk: e.matmul(
                            bank(bk)[:, (ci % 4) * 128:(ci % 4 + 1) * 128],
                            lhsT=F2[gsel * 64:(gsel + 1) * 64, c_, :], rhs=W1[gsel * 64:(gsel + 1) * 64, :],
                            start=True, stop=True), [("F2", gsel), "W1"], [("bk", bk)])
                    src = psall[:, grp * 2048:(grp + 1) * 2048].rearrange("p (c r) -> p r c", r=128)
                    dst = Y[:, gsel, :, c16 * 16:(c16 + 1) * 16]
                    rk = [("bk", 4 * grp + q_) for q_ in range(4)]
                    if grp == 1:
                        act(lambda e, src=src, dst=dst: e.copy(out=dst, in_=src), rk, [("Y", gsel)])
                    else:
                        dve(lambda e, src=src, dst=dst: e.tensor_copy(out=dst, in_=src), rk, [("Y", gsel)])
            for gsel in range(2):
                g = 2 * gp + gsel
                slot = gsel
                for bt in range(10):
                    a0 = bt * 7
                    na = min(7, 64 - a0)
                    zb = (2, 3, 6)[zc[0] % 3]
                    zc[0] += 1
                    for ai in range(na):
                        al = a0 + ai
                        pe(lambda e, gsel=gsel, al=al, ai=ai, zb=zb: e.matmul(
                            bank(zb)[:, ai * 72:(ai + 1) * 72], lhsT=Y[:, gsel, al, :], rhs=W3[:, al, 0, :],
                            start=True, stop=False), [("Y", gsel), "W3"], [("bk", zb)])
                        pe(lambda e, gsel=gsel, al=al, ai=ai, zb=zb: e.matmul(
                            bank(zb)[:, ai * 72:(ai + 1) * 72], lhsT=Y[:, gsel, 64 + al, :], rhs=W3[:, al, 1, :],
                            start=False, stop=True), [("Y", gsel), "W3"], [("bk", zb)])
                    bz = bank(zb)[:, 0:na * 72].rearrange("p (a x) -> p a x", x=72)
                    re_in = bz[:, :, 0:36].rearrange("p a b -> p b a")
                    im_in = bz[:, :, 36:72].rearrange("p a b -> p b a")
                    re_out = Vr[:, slot, :, a0:a0 + na]
                    im_out = Vi[:, slot, :, a0:a0 + na]
                    if zc[0] % 2 == 0:
                        act(lambda e, re_in=re_in, re_out=re_out: e.copy(out=re_out, in_=re_in), [("bk", zb)], [("Vr", slot)])
                        act(lambda e, im_in=im_in, im_out=im_out: e.copy(out=im_out, in_=im_in), [("bk", zb)], [("Vi", slot)])
                    else:
                        dve(lambda e, re_in=re_in, re_out=re_out: e.tensor_copy(out=re_out, in_=re_in), [("bk", zb)], [("Vr", slot)])
                        dve(lambda e, im_in=im_in, im_out=im_out: e.tensor_copy(out=im_out, in_=im_in), [("bk", zb)], [("Vi", slot)])
                for ch in range(5):
                    nb_ = 8 if ch < 4 else 4
                    cols = nb_ * 64
                    rb = 4 + (rc_[0] % 2)
                    rc_[0] += 1
                    vr = Vr[:, slot, 8 * ch:8 * ch + nb_, :].rearrange("p a b -> p (a b)")
                    vi = Vi[:, slot, 8 * ch:8 * ch + nb_, :].rearrange("p a b -> p (a b)")
                    pe(lambda e, rb=rb, cols=cols, vr=vr: e.matmul(bank(rb)[:, 0:cols], lhsT=W2[:, 0, :], rhs=vr,
                                                                   start=True, stop=False),
                       [("Vr", slot), "W2"], [("bk", rb)])
                    pe(lambda e, rb=rb, cols=cols, vi=vi: e.matmul(bank(rb)[:, 0:cols], lhsT=W2[:, 1, :], rhs=vi,
                                                                   start=False, stop=True),
                       [("Vi", slot), "W2"], [("bk", rb)])
                    dstR = Rt[:, g, 8 * ch * 64:8 * ch * 64 + cols]
                    if rc_[0] % 2 == 0:
                        act(lambda e, rb=rb, cols=cols, dstR=dstR: e.copy(out=dstR, in_=bank(rb)[:, 0:cols]), [("bk", rb)], ["R"])
                    else:
                        dve(lambda e, rb=rb, cols=cols, dstR=dstR: e.tensor_copy(out=dstR, in_=bank(rb)[:, 0:cols]),
                            [("bk", rb)], ["R"])
        if debug:
            sp(lambda e: e.dma_start(out=d_R, in_=Rt.rearrange("p a b -> p (a b)")), ["R"], ["d_R"])
        A.release()
        S_.barrier()

        A.mark()
        kT = A.alloc([128, NWIN * 128 + L], BF16)
        vA = A.alloc([128, NWIN + 2, 2, 65], BF16)
        qT = A.alloc([128, NQB, 4, 128], BF16)
        dve(lambda e: e.memset(vA, 1.0), [], ["vA"])
        A.mark()
        wWb = A.alloc([128, 8, WCOLS - GAOFF], BF16)
        for k in range(8):
            gq(lambda e, k=k: e.dma_start(out=wWb[:, k, :], in_=w_W[k * 128:(k + 1) * 128, GAOFF:WCOLS]), [], ["wWb"])
        cosT = A.alloc([128, NWIN * 128], F32)
        sinT = A.alloc([128, NWIN * 128], F32)
        sp(lambda e: e.dma_start(out=cosT, in_=t_cos), [], ["cos"])
        sp(lambda e: e.dma_start(out=sinT, in_=t_sin), [], ["sin"])
        xtW = Ring("xtW", [A.alloc([128, 1024], F32) for _ in range(4)])
        xnW = Ring("xnW", [A.alloc([128, 1024], BF16) for _ in range(2)])
        hTW = Ring("hTW", [A.alloc([128, 8, 128], BF16) for _ in range(8)])
        r1k = Ring("r1k", [A.alloc([128, 128], F32) for _ in range(2)])
        r2k = Ring("r2k", [A.alloc([128, 128], F32) for _ in range(2)])
        r1q = Ring("r1q", [A.alloc([128, 512], F32) for _ in range(2)])
        r2q = Ring("r2q", [A.alloc([128, 512], F32) for _ in range(2)])
        sgW = Ring("sgW", [A.alloc([128, 2048], BF16) for _ in range(3)])
        itemsW = [(ctxb[cb * 128:(cb + 1) * 128, :], CSC1, CSH1) for cb in range(2)]
        itemsW += [(x_win[wb * 128:(wb + 1) * 128, :], SC1, SH1) for wb in range(NWIN)]

        def w_info(i):
            if i < 2:
                return dict(ctx=True, kcol=NWIN * 128 + i * 128, vidx=NWIN + i, wb=None, isq=False)
            wb = i - 2
            return dict(ctx=False, kcol=wb * 128, vidx=wb, wb=wb, isq=(1 <= wb <= NQB))

        def w_c0(i):
            inf = w_info(i)
            for k in range(8):
                pe(lambda e, k=k: e.matmul(bank(2)[:, 0:128], lhsT=wWa[:, k, KOFF:KOFF + 128], rhs=hTW(i)[:, k, :],
                                           start=(k == 0), stop=(k == 7)), [hTW.key(i), "wWa"], [("bk", 2)])
            if not inf["ctx"]:
                for k in range(8):
                    pe(lambda e, k=k: e.matmul(bank(2)[:, 128:256], lhsT=wWa[:, k, KPOFF:KPOFF + 128],
                                               rhs=hTW(i)[:, k, :], start=(k == 0), stop=(k == 7)),
                       [hTW.key(i), "wWa"], [("bk", 2)])
            for k in range(8):
                pe(lambda e, k=k: e.matmul(bank(3)[:, 0:128], lhsT=hTW(i)[:, k, :], rhs=wWa[:, k, VOFF:VOFF + 128],
                                           start=(k == 0), stop=(k == 7)), [hTW.key(i), "wWa"], [("bk", 3)])

        def w_c1(i):
            inf = w_info(i)
            kcol = inf["kcol"]
            if inf["ctx"]:
                act(lambda e: e.copy(out=kT[:, kcol:kcol + 128], in_=bank(2)[:, 0:128]), [("bk", 2)], ["kT"])
            else:
                rc = inf["wb"] * 128
                dve(lambda e: e.tensor_tensor(out=r1k(i), in0=bank(2)[:, 0:128], in1=cosT[:, rc:rc + 128], op=ALU.mult),
                    [("bk", 2), "cos"], [r1k.key(i)])
                dve(lambda e: e.tensor_tensor(out=r2k(i), in0=bank(2)[:, 128:256], in1=sinT[:, rc:rc + 128], op=ALU.mult),
                    [("bk", 2), "sin"], [r2k.key(i)])
            vidx = inf["vidx"]
            act(lambda e: e.copy(out=vA[:, vidx, :, 0:64], in_=bank(3)[:, 0:128].rearrange("p (g d) -> p g d", d=64)),
                [("bk", 3)], ["vA"])
            if inf["isq"]:
                for (bkq, off) in ((4, QOFF), (5, QPOFF)):
                    for cq in range(4):
                        for k in range(8):
                            pe(lambda e, k=k, cq=cq, bkq=bkq, off=off: e.matmul(
                                bank(bkq)[:, cq * 128:(cq + 1) * 128],
                                lhsT=wWa[:, k, off + cq * 128:off + (cq + 1) * 128], rhs=hTW(i)[:, k, :],
                                start=(k == 0), stop=(k == 7)), [hTW.key(i), "wWa"], [("bk", bkq)])

        def w_c2(i):
            inf = w_info(i)
            kcol = inf["kcol"]
            if not inf["ctx"]:
                pool(lambda e: e.tensor_tensor(out=kT[:, kcol:kcol + 128], in0=r1k(i), in1=r2k(i), op=ALU.add),
                     [r1k.key(i), r2k.key(i)], ["kT"])
            if inf["isq"]:
                wb = inf["wb"]
                cs4 = cosT[:, wb * 128:(wb + 1) * 128].unsqueeze(1).broadcast_to([128, 4, 128])
                sn4 = sinT[:, wb * 128:(wb + 1) * 128].unsqueeze(1).broadcast_to([128, 4, 128])
                dve(lambda e: e.tensor_tensor(out=r1q(i).rearrange("p (c t) -> p c t", t=128),
                                              in0=bank(4).rearrange("p (c t) -> p c t", t=128), in1=cs4, op=ALU.mult),
                    [("bk", 4), "cos"], [r1q.key(i)])
                dve(lambda e: e.tensor_tensor(out=r2q(i).rearrange("p (c t) -> p c t", t=128),
                                              in0=bank(5).rearrange("p (c t) -> p c t", t=128), in1=sn4, op=ALU.mult),
                    [("bk", 5), "sin"], [r2q.key(i)])

        def gates_mm(i, gi):
            gb = 6 + (gi % 2)
            for k in range(8):
                pe(lambda e, k=k: e.matmul(bank(gb), lhsT=hTW(i)[:, k, :],
                                           rhs=wWb[:, k, gi * 512:(gi + 1) * 512],
                                           start=(k == 0), stop=(k == 7)), [hTW.key(i), "wWb"], [("bk", gb)])

        def gates_sig(i, gi):
            gb = 6 + (gi % 2)
            act(lambda e: e.activation(out=sgW(i)[:, gi * 512:(gi + 1) * 512], in_=bank(gb), func=AF.Sigmoid),
                [("bk", gb)], [sgW.key(i)])

        def w_c3(i):
            inf = w_info(i)
            if inf["isq"]:
                qi = inf["wb"] - 1
                pool(lambda e: e.tensor_tensor(out=qT[:, qi, :, :].rearrange("p c t -> p (c t)"), in0=r1q(i), in1=r2q(i),
                                               op=ALU.add), [r1q.key(i), r2q.key(i)], ["qT"])
                gates_mm(i, 0)
                gates_mm(i, 1)

        def w_c4(i):
            inf = w_info(i)
            if inf["isq"]:
                qi = inf["wb"] - 1
                gates_sig(i, 0)
                gates_sig(i, 1)
                gates_mm(i, 2)
                gates_mm(i, 3)
                gates_sig(i, 2)
                gates_sig(i, 3)
                sp(lambda e: e.dma_start(out=SG_dram[qi * 128:(qi + 1) * 128, :], in_=sgW(i)), [sgW.key(i)], ["SG_dram"])

        run_pipeline(len(itemsW), norm_stages(itemsW, xtW, xnW, hTW, (0, 1), 0, merge=True) + [w_c0, w_c1, w_c2, w_c3, w_c4])
        if debug:
            sp(lambda e: e.dma_start(out=d_kT, in_=kT), ["kT"], ["d_kT"])
            sp(lambda e: e.dma_start(out=d_qT, in_=qT.rearrange("p a b c -> p (a b c)")), ["qT"], ["d_qT"])
            sp(lambda e: e.dma_start(out=d_vA, in_=vA.rearrange("p a b c -> p (a b c)")), ["vA"], ["d_vA"])
        A.release()
        S_.barrier()

        A.top_lo = A.nbytes
        A.mark()
        wpa = A.alloc([128, 4, 1024], BF16)
        wpf = A.alloc([128, 4, 1024], BF16)
        wout = A.alloc([128, 8, 1024], BF16)
        for (dst, src, key) in ((wpa, w_pa, "wpa"), (wpf, w_pf, "wpf")):
            for k in range(4):
                gq(lambda e, k=k, dst=dst, src=src: e.dma_start(out=dst[:, k, :], in_=src[k * 128:(k + 1) * 128, :]),
                   [], [key])
        load_weight_k(wout, w_out, 1024, "wout")
        mask = A.alloc([128, 4, 512], BF16)
        sp(lambda e: e.dma_start(out=mask, in_=t_mask), [], ["mask"])
        PTr = [[[A.alloc([128, 512], BF16) for _ in range(5)] for _ in range(2)] for _ in range(2)]
        aoR = Ring("ao", [A.alloc([128, 8, 64], BF16) for _ in range(2)])
        aoTR = Ring("aoT", [A.alloc([128, 4, 128], BF16) for _ in range(3)])
        denR = Ring("den", [A.alloc([128, 8], F32) for _ in range(2)])
        sgtR = Ring("sgt", [A.alloc([128, 2048], BF16) for _ in range(3)])
        tA = [A.alloc([128, 512], F32) for _ in range(2)]
        tB = [A.alloc([128, 512], F32) for _ in range(2)]
        mbR = Ring("mb", [A.alloc([128, 1024], BF16) for _ in range(3)])
        mTR = Ring("mT", [A.alloc([128, 8, 128], BF16) for _ in range(2)])
        ygR = Ring("yg", [A.alloc([128, 1024], F32) for _ in range(2)])
        xthR = Ring("xth", [A.alloc([128, 1024], F32) for _ in range(3)])
        x1R_ = Ring("x1s", [A.alloc([128, 1024], F32) for _ in range(3)])
        xn2R = Ring("xn2", [A.alloc([128, 1024], BF16) for _ in range(2)])
        h2bR = Ring("h2b", [A.alloc([128, 8, 128], BF16) for _ in range(2)])
        spc = [0]

        def tq(i):
            return i + 1, i

        def t_att(i, g):
            qb, qi = tq(i)
            for jt in range(5):
                sb = spc[0] % 2
                spc[0] += 1
                kcol = (qb - 1 + jt) * 128 if jt < 3 else NWIN * 128 + (jt - 3) * 128
                mi = None
                if jt == 0:
                    mi = 2 if qb == 2 else 0
                elif jt == 2:
                    mi = 3 if qb == NQB - 1 else 1
                pe(lambda e, kcol=kcol, sb=sb, mi=mi: e.matmul(
                    bank(sb), lhsT=kT[g * 64:(g + 1) * 64, kcol:kcol + 128],
                    rhs=qT[g * 64:(g + 1) * 64, qi, :, :].rearrange("p c t -> p (c t)"),
                    start=True, stop=(mi is None)), ["kT", "qT"], [("bk", sb)])
                if mi is not None:
                    pe(lambda e, sb=sb, mi=mi: e.matmul(bank(sb), lhsT=idb, rhs=mask[:, mi, :], start=False, stop=True),
                       ["idb", "mask"], [("bk", sb)])
                dstp = PTr[i % 2][g][jt]
                act(lambda e, sb=sb, dstp=dstp: e.activation(out=dstp, in_=bank(sb), func=AF.Exp, scale=0.125),
                    [("bk", sb)], [("PT", i % 2, g, jt)])

        def t1_(i):
            t_att(i, 0)

        def t2_(i):
            t_att(i, 1)

        def t_pv(i, g):
            qb, qi = tq(i)
            ob = 2
            for hh in range(4):
                for jt in range(5):
                    vidx = (qb - 1 + jt) if jt < 3 else NWIN + (jt - 3)
                    src = PTr[i % 2][g][jt]
                    pe(lambda e, hh=hh, vidx=vidx, src=src, jt=jt: e.matmul(
                        bank(ob)[:, hh * 65:(hh + 1) * 65], lhsT=src[:, hh * 128:(hh + 1) * 128],
                        rhs=vA[:, vidx, g, :], start=(jt == 0), stop=(jt == 4)),
                       [("PT", i % 2, g, jt), "vA"], [("bk", ob)])
            ov = bank(ob)[:, 0:260].rearrange("p (h d) -> p h d", d=65)
            dn = denR(i)[:, 4 * g:4 * g + 4]
            dve(lambda e: e.tensor_tensor(out=dn, in0=ov[:, :, 64], in1=dv[:, ESINK, 4 * g:4 * g + 4], op=ALU.add),
                [("bk", ob), ("dv", ESINK)], [("den", i % 2, g)])
            dve(lambda e: e.reciprocal(out=dn, in_=dn), [("den", i % 2, g)], [("den", i % 2, g)])
            dve(lambda e: e.tensor_tensor(out=aoR(i)[:, 4 * g:4 * g + 4, :], in0=ov[:, :, 0:64],
                                          in1=dn.unsqueeze(2).broadcast_to([128, 4, 64]), op=ALU.mult),
                [("bk", ob), ("den", i % 2, g)], [("ao", i % 2, g)])

        def t3_(i):
            t_pv(i, 0)

        def t4_(i):
            qb, qi = tq(i)
            sp(lambda e: e.dma_start(out=sgtR(i), in_=SG_dram[qi * 128:(qi + 1) * 128, :]), ["SG_dram"], [sgtR.key(i)])
            t_pv(i, 1)

        def t5_(i):
            aof = aoR(i).rearrange("p h d -> p (h d)")
            for cj in range(4):
                pe(lambda e, cj=cj: e.transpose(out=bankb(4)[:, cj * 128:(cj + 1) * 128],
                                                in_=aof[:, cj * 128:(cj + 1) * 128], identity=idb),
                   [("ao", i % 2, 0), ("ao", i % 2, 1), "idb"], [("bk", 4)])
            act(lambda e: e.copy(out=aoTR(i).rearrange("p c t -> p (c t)"), in_=bankb(4)[:, 0:512]),
                [("bk", 4)], [aoTR.key(i)])

        def t_merge(i, half):
            qb, qi = tq(i)
            for cj in range(4):
                pe(lambda e, cj=cj: e.matmul(bank(5), lhsT=aoTR(i)[:, cj, :], rhs=wpa[:, cj, half * 512:(half + 1) * 512],
                                             start=(cj == 0), stop=(cj == 3)), [aoTR.key(i), "wpa"], [("bk", 5)])
            for g in range(4):
                pe(lambda e, g=g: e.matmul(bank(6), lhsT=Rt[:, g, qi * 128:(qi + 1) * 128],
                                           rhs=wpf[:, g, half * 512:(half + 1) * 512],
                                           start=(g == 0), stop=(g == 3)), ["R", "wpf"], [("bk", 6)])
            hs = slice(half * 512, (half + 1) * 512)
            hs2 = slice(1024 + half * 512, 1024 + (half + 1) * 512)
            dve(lambda e: e.tensor_tensor(out=tA[half], in0=bank(5), in1=sgtR(i)[:, hs], op=ALU.mult),
                [("bk", 5), sgtR.key(i)], [("tA", half)])
            dve(lambda e: e.tensor_tensor(out=tB[half], in0=bank(6), in1=sgtR(i)[:, hs2], op=ALU.mult),
                [("bk", 6), sgtR.key(i)], [("tB", half)])
            pool(lambda e: e.tensor_tensor(out=mbR(i)[:, hs], in0=tA[half], in1=tB[half], op=ALU.add),
                 [("tA", half), ("tB", half)], [("mb", i % 3, half)])

        def t6_(i):
            t_merge(i, 0)

        def t7_(i):
            t_merge(i, 1)

        def t8_(i):
            for k in range(8):
                pe(lambda e, k=k: e.transpose(out=bankb(4)[:, k * 128:(k + 1) * 128],
                                              in_=mbR(i)[:, k * 128:(k + 1) * 128], identity=idb),
                   [("mb", i % 3, k // 4), "idb"], [("bk", 4)])
            act(lambda e: e.copy(out=mTR(i).rearrange("p c t -> p (c t)"), in_=bankb(4)), [("bk", 4)], [mTR.key(i)])

        def ycol(i):
            return 8 + 4 * (i % 4)

        def t9_(i):
            qb, qi = tq(i)
            sp(lambda e: e.dma_start(out=xthR(i), in_=x_win[qb * 128:(qb + 1) * 128, :]), [], [xthR.key(i)])
            c = ycol(i)
            for half in range(2):
                yb = (3, 7)[half]
                hs = slice(half * 512, (half + 1) * 512)
                for k in range(8):
                    pe(lambda e, k=k, yb=yb, half=half: e.matmul(
                        bank(yb), lhsT=mTR(i)[:, k, :], rhs=wout[:, k, half * 512:(half + 1) * 512],
                        start=(k == 0), stop=(k == 7)), [mTR.key(i), "wout"], [("bk", yb)])
                act(lambda e, yb=yb, half=half: e.activation(out=junk[:, 0:512], in_=bank(yb), func=AF.Square,
                                                             accum_out=small[:, c + half:c + half + 1]),
                    [("bk", yb)], [("sm", c + half), "junk"])
                dve(lambda e, yb=yb, hs=hs: e.tensor_tensor(out=ygR(i)[:, hs], in0=bank(yb), in1=gg1[:, hs], op=ALU.mult),
                    [("bk", yb), "gg1", ("sm", c + half)], [("yg", i % 2, half)])

        def t10_(i):
            c = ycol(i)
            rsy = small[:, c + 2:c + 3]
            dve(lambda e: e.tensor_tensor(out=rsy, in0=small[:, c:c + 1], in1=small[:, c + 1:c + 2], op=ALU.add),
                [("sm", c), ("sm", c + 1)], [("sm", c + 2)])
            act(lambda e: e.activation(out=rsy, in_=rsy, func=AF.Sqrt, scale=1.0 / D, bias=eps_t[:, 0:1]),
                [("sm", c + 2), "eps"], [("sm", c + 2)])

        def t11_(i):
            qb, qi = tq(i)
            c = ycol(i)
            rsy = small[:, c + 2:c + 3]
            dve(lambda e: e.reciprocal(out=rsy, in_=rsy), [("sm", c + 2)], [("sm", c + 2)])
            for half in range(2):
                hs = slice(half * 512, (half + 1) * 512)
                dve(lambda e, hs=hs: e.scalar_tensor_tensor(out=x1R_(i)[:, hs], in0=ygR(i)[:, hs], scalar=rsy,
                                                            in1=xthR(i)[:, hs], op0=ALU.mult, op1=ALU.add),
                    [("yg", i % 2, half), ("sm", c + 2), xthR.key(i)], [("x1s", i % 3, half)])
            if 2 <= qb <= NQB - 1:
                sp(lambda e: e.dma_start(out=X1_dram[(qb - 2) * 128:(qb - 1) * 128, :], in_=x1R_(i)),
                   [("x1s", i % 3, 0), ("x1s", i % 3, 1)], ["X1_dram"])

        def t12_(i):
            c = ycol(i)
            act(lambda e: e.activation(out=junk, in_=x1R_(i), func=AF.Square, accum_out=small[:, c + 3:c + 4]),
                [("x1s", i % 3, 0), ("x1s", i % 3, 1)], [("sm", c + 3), "junk"])
            act(lambda e: e.activation(out=small[:, c + 3:c + 4], in_=small[:, c + 3:c + 4], func=AF.Sqrt,
                                       scale=1.0 / D, bias=eps_t[:, 0:1]), [("sm", c + 3), "eps"], [("sm", c + 3)])

        def t13_(i):
            c = ycol(i)
            dve(lambda e: e.reciprocal(out=small[:, c + 3:c + 4], in_=small[:, c + 3:c + 4]), [("sm", c + 3)], [("sm", c + 3)])
            act(lambda e: e.activation(out=xn2R(i), in_=x1R_(i), func=AF.Copy, scale=small[:, c + 3:c + 4]),
                [("x1s", i % 3, 0), ("x1s", i % 3, 1), ("sm", c + 3)], [xn2R.key(i)])

        def t14_(i):
            qb, qi = tq(i)
            own = 2 <= qb <= NQB - 1
            sci, shi = (SC2, SH2) if own else ((SC2L, SH2L) if qb == 1 else (SC2R, SH2R))
            for k in range(8):
                pe(lambda e, k=k: e.transpose(out=bankb(4)[:, k * 128:(k + 1) * 128],
                                              in_=xn2R(i)[:, k * 128:(k + 1) * 128], identity=idb),
                   [xn2R.key(i), "idb"], [("bk", 4)])
            for k in range(8):
                src = bankb(4)[:, k * 128:(k + 1) * 128]
                dst = h2bR(i)[:, k, :]
                if i % 2 == 0:
                    dve(lambda e, src=src, dst=dst, k=k: e.tensor_scalar(
                        out=dst, in0=src, scalar1=dv[:, sci, k:k + 1], scalar2=dv[:, shi, k:k + 1],
                        op0=ALU.mult, op1=ALU.add), [("bk", 4), ("dv", sci), ("dv", shi)], [h2bR.key(i)])
                else:
                    act(lambda e, src=src, dst=dst, k=k: e.activation(
                        out=dst, in_=src, func=AF.Identity, scale=dv[:, sci, k:k + 1], bias=dv[:, shi, k:k + 1]),
                        [("bk", 4), ("dv", sci), ("dv", shi)], [h2bR.key(i)])
            if own:
                c0 = (qb - 2) * 128
                sp(lambda e: e.dma_start(out=H2_dram[:, :, c0:c0 + 128], in_=h2bR(i)), [h2bR.key(i)], ["H2_dram"])
            elif qb == 1:
                dve(lambda e: e.tensor_copy(out=h2halo[:, :, 0:1], in_=h2bR(i)[:, :, 127:128]), [h2bR.key(i)], ["h2halo"])
            else:
                dve(lambda e: e.tensor_copy(out=h2halo[:, :, 1:2], in_=h2bR(i)[:, :, 0:1]), [h2bR.key(i)], ["h2halo"])

        def tA_(i):
            t1_(i)
            t2_(i)

        def tB_(i):
            t3_(i)
            t4_(i)

        def tC_(i):
            t10_(i)
            t11_(i)

        def tD_(i):
            t12_(i)
            t13_(i)

        def tM_(i):
            t6_(i)
            t7_(i)

        run_pipeline(NQB, [tA_, tB_, t5_, tM_, t8_, t9_, tC_, tD_, t14_])
        A.release()
        A.release()
        S_.barrier()

        A.mark()
        h2T = A.alloc([128, 8, 2050], BF16)
        sp(lambda e: e.dma_start(out=h2T[:, :, 1:2049], in_=H2_dram), ["H2_dram"], ["h2T"])
        dve(lambda e: e.tensor_copy(out=h2T[:, :, 0:1], in_=h2halo[:, :, 0:1]), ["h2halo"], ["h2T"])
        dve(lambda e: e.tensor_copy(out=h2T[:, :, 2049:2050], in_=h2halo[:, :, 1:2]), ["h2halo"], ["h2T"])
        x1c = [A.alloc([128, 1024], F32) for _ in range(4)]
        actb = A.alloc([128, NJ, 1024], BF16)
        wuR = Ring("wu", [A.alloc([128, 8, 128], BF16) for _ in range(3)])
        wgR = Ring("wg", [A.alloc([128, 8, 128], BF16) for _ in range(4)])
        wdR = Ring("wd", [A.alloc([128, 512], BF16) for _ in range(6)])
        ysb = A.alloc([128, 8, 512], F32)
        ysb2 = A.alloc([128, 8, 512], F32)
        ubR = Ring("ub", [A.alloc([128, 514], F32) for _ in range(4)])
        c1R = Ring("c1", [A.alloc([128, 512], F32) for _ in range(2)])
        c3R = Ring("c3", [A.alloc([128, 512], F32) for _ in range(2)])
        silR = Ring("sil", [A.alloc([128, 512], F32) for _ in range(2)])
        x1R = Ring("x1c", x1c)
        CW, CB = 96, 162
        nD = 0
        nO = 0
        for hf in range(2):
            def cinfo(n, hf=hf):
                j, st = n // 2, n % 2
                return j, st, 1 + hf * 1024 + st * 512

            def u0(n, hf=hf):
                j, st, c0 = cinfo(n)
                if st == 0:
                    gq(lambda e: e.dma_start(out=wuR(j), in_=w_upr[j, 0]), [], [wuR.key(j)])
                    gq(lambda e: e.dma_start(out=wgR(j), in_=w_upr[j, 1]), [], [wgR.key(j)])

            def u1(n, hf=hf):
                j, st, c0 = cinfo(n)
                ub, hb = (0, 4) if n % 2 == 0 else (1, 5)
                for k in range(8):
                    pe(lambda e, k=k: e.matmul(bank(ub), lhsT=wuR(j)[:, k, :], rhs=h2T[:, k, c0:c0 + 512],
                                               start=(k == 0), stop=(k == 7)), [wuR.key(j), "h2T"], [("bk", ub)])
                for k in range(8):
                    pe(lambda e, k=k: e.matmul(bank(hb)[:, 0:2], lhsT=wuR(j)[:, k, :],
                                               rhs=h2T[:, k, c0 - 1:c0 + 513:513],
                                               start=(k == 0), stop=(k == 7)), [wuR.key(j), "h2T"], [("bk", hb)])

            def u2(n, hf=hf):
                j, st, c0 = cinfo(n)
                ub, hb = (0, 4) if n % 2 == 0 else (1, 5)
                act(lambda e: e.copy(out=ubR(n)[:, 1:513], in_=bank(ub)), [("bk", ub)], [ubR.key(n)])
                dve(lambda e: e.tensor_copy(out=ubR(n)[:, 0:514:513], in_=bank(hb)[:, 0:2]), [("bk", hb)], [ubR.key(n)])

            def u3(n, hf=hf):
                j, st, c0 = cinfo(n)
                act(lambda e: e.activation(out=c1R(n), in_=ubR(n)[:, 1:513], func=AF.Identity,
                                           scale=vec[:, CW + 22 + j:CW + 23 + j], bias=vec[:, CB + j:CB + j + 1]),
                    [ubR.key(n), "vec"], [c1R.key(n)])

            def u4(n, hf=hf):
                j, st, c0 = cinfo(n)
                dve(lambda e: e.scalar_tensor_tensor(out=c1R(n), in0=ubR(n)[:, 0:512], scalar=vec[:, CW + j:CW + j + 1],
                                                     in1=c1R(n), op0=ALU.mult, op1=ALU.add),
                    [ubR.key(n), c1R.key(n), "vec"], [c1R.key(n)])
                dve(lambda e: e.scalar_tensor_tensor(out=c3R(n), in0=ubR(n)[:, 2:514],
                                                     scalar=vec[:, CW + 44 + j:CW + 45 + j], in1=c1R(n),
                                                     op0=ALU.mult, op1=ALU.add),
                    [ubR.key(n), c1R.key(n), "vec"], [c3R.key(n)])

            def u5(n, hf=hf):
                j, st, c0 = cinfo(n)
                gb = 2 + (n % 2)
                act(lambda e: e.activation(out=silR(n), in_=c3R(n), func=AF.Silu), [c3R.key(n)], [silR.key(n)])
                for k in range(8):
                    pe(lambda e, k=k: e.matmul(bank(gb), lhsT=wgR(j)[:, k, :], rhs=h2T[:, k, c0:c0 + 512],
                                               start=(k == 0), stop=(k == 7)), [wgR.key(j), "h2T"], [("bk", gb)])

            def u6(n, hf=hf):
                j, st, c0 = cinfo(n)
                gb = 2 + (n % 2)
                dve(lambda e: e.tensor_tensor(out=actb[:, j, st * 512:(st + 1) * 512], in0=silR(n), in1=bank(gb),
                                              op=ALU.mult), [silR.key(n), ("bk", gb)], [("actb", j)])

            run_pipeline(2 * NJ, [u0, u1, u2, u3, u4, u5, u6])

            for dh in range(2):
                for j in range(NJ):
                    dsl = nD
                    nD += 1
                    gq(lambda e, j=j, dsl=dsl, dh=dh: e.dma_start(
                        out=wdR(dsl), in_=w_down[j * 128:(j + 1) * 128, dh * 512:(dh + 1) * 512]), [], [wdR.key(dsl)])
                    for tt in range(8):
                        pe(lambda e, j=j, dsl=dsl, tt=tt: e.matmul(
                            bank(tt), lhsT=actb[:, j, tt * 128:(tt + 1) * 128], rhs=wdR(dsl),
                            start=(j == 0), stop=(j == NJ - 1)), [("actb", j), wdR.key(dsl)], [("bk", tt)])
                ydst = ysb if dh == 0 else ysb2
                for tt in range(8):
                    act(lambda e, tt=tt, dh=dh: e.activation(out=junk[:, 0:512], in_=bank(tt), func=AF.Square,
                                                             accum_out=small[:, 16 + 8 * dh + tt:17 + 8 * dh + tt]),
                        [("bk", tt)], [("sm", 16 + 8 * dh + tt), "junk"])
                    dve(lambda e, tt=tt, dh=dh, ydst=ydst: e.tensor_tensor(
                        out=ydst[:, tt, :], in0=bank(tt), in1=gg2[:, dh * 512:(dh + 1) * 512], op=ALU.mult),
                        [("bk", tt), "gg2"], [("ysb", dh, tt)])

            def q0(tt, hf=hf):
                tile_i = hf * 8 + tt
                sp(lambda e: e.dma_start(out=x1R(tile_i), in_=X1_dram[tile_i * 128:(tile_i + 1) * 128, :]),
                   ["X1_dram"], [x1R.key(tile_i)])

            def q1(tt, hf=hf):
                rf = small[:, 32 + tt:33 + tt]
                dve(lambda e: e.tensor_tensor(out=rf, in0=small[:, 16 + tt:17 + tt], in1=small[:, 24 + tt:25 + tt], op=ALU.add),
                    [("sm", 16 + tt), ("sm", 24 + tt)], [("sm", 32 + tt)])
                act(lambda e: e.activation(out=rf, in_=rf, func=AF.Sqrt, scale=1.0 / D, bias=eps_t[:, 0:1]),
                    [("sm", 32 + tt), "eps"], [("sm", 32 + tt)])

            def q2(tt, hf=hf):
                tile_i = hf * 8 + tt
                rf = small[:, 32 + tt:33 + tt]
                dve(lambda e: e.reciprocal(out=rf, in_=rf), [("sm", 32 + tt)], [("sm", 32 + tt)])
                for dh_, ysrc in ((0, ysb), (1, ysb2)):
                    hs = slice(dh_ * 512, (dh_ + 1) * 512)
                    dve(lambda e, hs=hs, ysrc=ysrc: e.scalar_tensor_tensor(
                        out=x1R(tile_i)[:, hs], in0=ysrc[:, tt, :], scalar=rf, in1=x1R(tile_i)[:, hs],
                        op0=ALU.mult, op1=ALU.add),
                        [("ysb", dh_, tt), ("sm", 32 + tt), x1R.key(tile_i)], [x1R.key(tile_i)])

            def q3(tt, hf=hf):
                tile_i = hf * 8 + tt
                sp(lambda e: e.dma_start(out=out[tile_i * 128:(tile_i + 1) * 128, :], in_=x1R(tile_i)),
                   [x1R.key(tile_i)], [("out", tile_i)])

            run_pipeline(8, [q0, q1, q2, q3])
        if debug:
            sp(lambda e: e.dma_start(out=d_hl, in_=h2halo.rearrange("p a b -> p (a b)")), ["h2halo"], ["d_hl"])
        S_.barrier()
        S_.emit(nc, block, sems)
    return nc


def _bf(a):
    return np.ascontiguousarray(a.astype(np.float32)).astype(ml_dtypes.bfloat16)


def _partner(d):
    return d + 16 if (d % 32) < 16 else d - 16


def _const_tables():
    a = np.arange(64, dtype=np.float64)
    al = np.arange(64, dtype=np.float64)
    ang = 2 * np.pi * np.outer(a, al) / 64.0
    w1h = np.concatenate([np.cos(ang), -np.sin(ang)], axis=1)
    w1 = np.concatenate([w1h, w1h], axis=0)
    c = np.arange(128, dtype=np.float64)
    ang = 2 * np.pi * np.outer(c, c) / 128.0
    C, Sn = np.cos(ang), np.sin(ang)
    w2 = np.stack([C, Sn], axis=1)
    i = np.arange(128)
    mL = np.where(i[:, None] >= i[None, :], 0.0, NEGM)
    mR = np.where(i[:, None] <= i[None, :], 0.0, NEGM)
    mN = np.full((128, 128), NEGM)
    return w1, w2, mL, mR, mN


def _core_tables(r):
    n0 = 16 * r
    b = np.arange(128, dtype=np.float64)[:, None, None]
    al = np.arange(64, dtype=np.float64)[None, :, None]
    j = np.arange(36)
    beta = ((2 * (n0 - 1) + j) % 128).astype(np.float64)[None, None, :]
    th = 2 * np.pi * b * (al + 64.0 * beta) / 8192.0
    w3 = np.stack([np.concatenate([np.cos(th), -np.sin(th)], axis=2),
                   np.concatenate([np.sin(th), np.cos(th)], axis=2)], axis=2) / 1024.0
    t0 = 2048 * r
    t = np.arange(NWIN * 128) + t0 - 256
    row = (t // 64).astype(np.float32)
    col = (t % 64).astype(np.float32)
    inv_freq = (np.float32(10000.0) ** (-np.arange(0, 32, 2, dtype=np.float32) / np.float32(32))).astype(np.float32)
    cosT = np.zeros((64, NWIN * 128), np.float32)
    sinT = np.zeros((64, NWIN * 128), np.float32)
    for d in range(64):
        pos = row if d < 32 else col
        angf = (pos * inv_freq[d % 16]).astype(np.float32)
        cosT[d] = np.cos(angf).astype(np.float32)
        sg = -1.0 if (d % 32) < 16 else 1.0
        sinT[d] = (sg * np.sin(angf)).astype(np.float32)
    cosT = np.concatenate([cosT, cosT], axis=0)
    sinT = np.concatenate([sinT, sinT], axis=0)
    return w3, cosT, sinT


_NC_CACHE = {}


def kernel(x, c, ctx, c_ctx, w_mod, b_mod, g_pre1, g_post1, g_pre2, g_post2,
           w_in, sink, w_pa, w_pf, w_out, w_up, conv_w, conv_b, w_down):
    f32 = np.float32
    x = np.asarray(x, f32)
    c = np.asarray(c, f32)
    ctx = np.asarray(ctx, f32)
    c_ctx = np.asarray(c_ctx, f32)
    w_in0 = np.asarray(w_in, f32)[0]
    w_up0 = np.asarray(w_up, f32)[0]

    qcols, qpcols = [], []
    for cq in range(4):
        for h in (cq, 4 + cq):
            for d in range(64):
                qcols.append(h * 64 + d)
                qpcols.append(h * 64 + _partner(d))
    kcols = [512 + g * 64 + d for g in range(2) for d in range(64)]
    kpcols = [512 + g * 64 + _partner(d) for g in range(2) for d in range(64)]
    vcols = list(range(640, 768))
    gcols = list(range(1280, 3328))
    w_W = np.ascontiguousarray(w_in0[:, qcols + qpcols + kcols + kpcols + vcols + gcols])
    assert w_W.shape[1] == WCOLS
    w_f = np.ascontiguousarray(w_in0[:, 768:1280])
    wu = w_up0.reshape(8, 128, 2, NJ, 128)
    w_upr = np.ascontiguousarray(wu.transpose(3, 2, 1, 0, 4))

    w1, w2, mL, mR, mN = _const_tables()
    idb = _bf(np.eye(128))
    idf = np.eye(128, dtype=f32)

    def tile4(m):
        return np.tile(m, (1, 4))

    in_maps = []
    for core in range(8):
        b, r = core // 4, core % 4
        t0 = 2048 * r
        xw = np.zeros((NWIN * 128, D), f32)
        lo, hi = t0 - 256, t0 + 2048 + 256
        slo, shi = max(lo, 0), min(hi, S)
        xw[slo - lo:shi - lo] = x[b, slo:shi]
        vecs = np.zeros((128, NVEC), f32)
        ct = c[b].reshape(8, 128).T
        cct = c_ctx.reshape(8, 128).T
        vecs[:, 0:16] = np.stack([ct, cct], axis=2).reshape(128, 16)
        vecs[:, 16:64] = np.asarray(b_mod, f32)[0].reshape(48, 128).T
        vecs[:, 64:72] = np.asarray(g_pre1, f32)[0].reshape(8, 128).T
        vecs[:, 72:80] = np.asarray(g_post1, f32)[0].reshape(8, 128).T
        vecs[:, 80:88] = np.asarray(g_pre2, f32)[0].reshape(8, 128).T
        vecs[:, 88:96] = np.asarray(g_post2, f32)[0].reshape(8, 128).T
        cw = np.asarray(conv_w, f32)[0]
        vecs[:, 96:162] = cw.reshape(3, NJ, 128).transpose(2, 0, 1).reshape(128, 66)
        vecs[:, 162:184] = np.asarray(conv_b, f32)[0].reshape(NJ, 128).T
        vecs[:, 184:192] = np.broadcast_to(np.asarray(sink, f32)[0][None, :], (128, 8))
        vecs[:, 192] = 1.0 if r > 0 else 0.0
        vecs[:, 193] = 1.0 if r < 3 else 0.0
        w3, cosT, sinT = _core_tables(r)
        masks = np.stack([tile4(mL), tile4(mR), tile4(mN if r == 0 else mL), tile4(mN if r == 3 else mR)], axis=1)
        in_maps.append({
            "x_full": np.ascontiguousarray(x[b]),
            "x_win": xw,
            "ctxb": np.ascontiguousarray(ctx[b]),
            "vecs": vecs,
            "w_mod": np.ascontiguousarray(np.asarray(w_mod, f32)[0]),
            "w_f": w_f,
            "w_W": w_W,
            "w_pa": np.ascontiguousarray(np.asarray(w_pa, f32)[0]),
            "w_pf": np.ascontiguousarray(np.asarray(w_pf, f32)[0]),
            "w_out": np.ascontiguousarray(np.asarray(w_out, f32)[0]),
            "w_upr": w_upr,
            "w_down": np.ascontiguousarray(np.asarray(w_down, f32)[0]),
            "t_w1": _bf(w1),
            "t_w2": _bf(w2),
            "t_w3": _bf(w3),
            "t_cos": cosT,
            "t_sin": sinT,
            "t_mask": _bf(masks),
            "t_idb": idb,
            "t_idf": idf,
        })
    dbgmode = bool(_NC_CACHE.get("debug"))
    key = "nc_dbg" if dbgmode else "nc"
    if key not in _NC_CACHE:
        _NC_CACHE[key] = build_program(debug=dbgmode)
    nc = _NC_CACHE[key]
    res = run_bass_kernel_spmd(nc, in_maps, core_ids=list(range(8)))
    if dbgmode:
        _NC_CACHE["last_results"] = res.results
    outp = np.zeros((2, S, D), f32)
    for core in range(8):
        b, r = core // 4, core % 4
        outp[b, 2048 * r:2048 * (r + 1)] = np.asarray(res.results[core]["out"], f32)
    return outp
```
